# Optimizing a Trainium2 kernel written in Bass

```python
import math
import jax, jax.numpy as jnp
from jax import lax
import numpy as np

D_MODEL = 2048
BATCH = 4
SEQ = 2048
DEPTH = 2
DEC_BATCH = 128
DEC_SEQ = 4
PAST_LEN = 8192
PAGE_SIZE = 128

EPS = 1e-6
D_FF = 5632

ML_HEADS = 4
ML_DQK = D_MODEL // 16
ML_DV = D_MODEL // 8
ML_CHUNK = 64

MLA_HEADS = 8
MLA_NOPE = 128
MLA_ROPE = 64
MLA_V = 128
MLA_Q_RANK = D_MODEL // 4
MLA_KV_RANK = D_MODEL // 4
MLA_SCALE = (MLA_NOPE + MLA_ROPE) ** -0.5
ROPE_BASE = 10000.0
Q_BLOCK = 128

EVEN_SPLITS = (ML_HEADS * ML_DQK, ML_HEADS * ML_DQK, ML_HEADS * ML_DV, ML_HEADS, ML_HEADS,
               ML_HEADS * ML_DV, MLA_Q_RANK, MLA_KV_RANK, MLA_ROPE)
EVEN_IN = 2 * ML_HEADS * ML_DQK + 2 * ML_HEADS * ML_DV + 2 * ML_HEADS + MLA_Q_RANK + MLA_KV_RANK + MLA_ROPE
EVEN_OUT = ML_HEADS * ML_DV + MLA_HEADS * MLA_V

SSM_D_INNER = 2 * D_MODEL
SSM_HEADDIM = 64
SSM_HEADS = SSM_D_INNER // SSM_HEADDIM
SSM_GROUPS = 8
SSM_STATE = 128
SSM_CONV = 4
SSM_CHUNK = 64
SSM_CONV_DIM = SSM_D_INNER + 2 * SSM_GROUPS * SSM_STATE
SSM_IN = SSM_D_INNER + SSM_CONV_DIM + SSM_HEADS

kernel_name = 'hybrid_mlstm_mla_mamba2_macaron_step'


def rmsnorm(x, g):
    xf = x.astype(jnp.float32)
    y = xf * lax.rsqrt(jnp.mean(xf * xf, axis=-1, keepdims=True) + EPS)
    return (y * g.astype(jnp.float32)).astype(x.dtype)


def swiglu(x, w_gate, w_up, w_down):
    return (jax.nn.silu(x @ w_gate) * (x @ w_up)) @ w_down


def split_cols(a, sizes):
    idx = np.cumsum(np.array(sizes))[:-1].tolist()
    return jnp.split(a, idx, axis=-1)


def rope(x, pos):
    half = MLA_ROPE // 2
    inv = ROPE_BASE ** (-jnp.arange(half, dtype=jnp.float32) / half)
    ang = pos.astype(jnp.float32)[:, None] * inv[None, :]
    cos = jnp.cos(ang)[:, None, :]
    sin = jnp.sin(ang)[:, None, :]
    xf = x.astype(jnp.float32)
    x1, x2 = xf[..., :half], xf[..., half:]
    return jnp.concatenate([x1 * cos - x2 * sin, x1 * sin + x2 * cos], axis=-1).astype(x.dtype)


def mlstm_chunkwise(q, k, v, logi, logf, C0, n0, m0):
    bsz, L = q.shape[0], q.shape[1]
    lc = math.gcd(L, ML_CHUNK)
    nc = L // lc
    f32 = jnp.float32

    def chunks(a):
        return a.astype(f32).reshape((bsz, nc, lc) + a.shape[2:]).swapaxes(0, 1)

    causal = jnp.tril(jnp.ones((lc, lc), dtype=bool))

    def step(carry, inp):
        C, n, m = carry
        qt, kt, vt, it, ft = inp
        b = jnp.cumsum(ft, axis=1).transpose(0, 2, 1)
        ih = it.transpose(0, 2, 1)
        dmat = jnp.where(causal, b[..., :, None] - b[..., None, :] + ih[..., None, :], -jnp.inf)
        inter = b + m[..., None]
        m_t = jnp.maximum(inter, jnp.max(dmat, axis=-1))
        w_inter = jnp.exp(inter - m_t)
        s = jnp.einsum('bthd,bshd->bhts', qt, kt) * jnp.exp(dmat - m_t[..., None])
        num = jnp.einsum('bhts,bshv->bhtv', s, vt) + w_inter[..., None] * jnp.einsum('bhvd,bthd->bhtv', C, qt)
        den = jnp.sum(s, axis=-1) + w_inter * jnp.einsum('bhd,bthd->bht', n, qt)
        h = num / jnp.maximum(jnp.abs(den), jnp.exp(-m_t))[..., None]
        b_last = b[..., -1]
        g = b_last[..., None] - b + ih
        m_new = jnp.maximum(b_last + m, jnp.max(g, axis=-1))
        decay = jnp.exp(b_last + m - m_new)
        wg = jnp.exp(g - m_new[..., None])
        C = decay[..., None, None] * C + jnp.einsum('bhs,bshv,bshd->bhvd', wg, vt, kt)
        n = decay[..., None] * n + jnp.einsum('bhs,bshd->bhd', wg, kt)
        return (C, n, m_new), h.transpose(0, 2, 1, 3)

    init = (C0.astype(f32), n0.astype(f32), m0.astype(f32))
    (C, n, m), hs = lax.scan(step, init, (chunks(q), chunks(k), chunks(v), chunks(logi), chunks(logf)))
    return hs.swapaxes(0, 1).reshape(bsz, L, ML_HEADS, ML_DV), C, n, m


def ssd_chunked(x, dt, A, Bm, Cm, h0):
    bsz, L = x.shape[0], x.shape[1]
    G, HG = SSM_GROUPS, SSM_HEADS // SSM_GROUPS
    lc = math.gcd(L, SSM_CHUNK)
    nc = L // lc
    f32 = jnp.float32

    def chunks(a, tail):
        return a.astype(f32).reshape((bsz, nc, lc) + tail).swapaxes(0, 1)

    xs = chunks(x, (G, HG, SSM_HEADDIM))
    dts = chunks(dt, (G, HG))
    bs = chunks(Bm, (G, SSM_STATE))
    cs = chunks(Cm, (G, SSM_STATE))
    a_g = A.astype(f32).reshape(G, HG)
    causal = jnp.tril(jnp.ones((lc, lc), dtype=bool))[None, :, :, None, None]

    def step(h, inp):
        xt, dtt, bt, ct = inp
        cum = jnp.cumsum(dtt * a_g, axis=1)
        seg = jnp.where(causal, cum[:, :, None] - cum[:, None, :], -jnp.inf)
        w = jnp.einsum('btgn,bsgn->btsg', ct, bt)[..., None] * jnp.exp(seg) * dtt[:, None]
        y = (jnp.einsum('btsgh,bsghp->btghp', w, xt)
             + jnp.einsum('btgn,bghpn->btghp', ct, h) * jnp.exp(cum)[..., None])
        decay = jnp.exp(cum[:, -1:] - cum) * dtt
        h = jnp.exp(cum[:, -1])[..., None, None] * h + jnp.einsum('bsgh,bsgn,bsghp->bghpn', decay, bt, xt)
        return h, y

    h0g = h0.astype(f32).reshape(bsz, G, HG, SSM_HEADDIM, SSM_STATE)
    h, ys = lax.scan(step, h0g, (xs, dts, bs, cs))
    return (ys.swapaxes(0, 1).reshape(bsz, L, SSM_HEADS, SSM_HEADDIM),
            h.reshape(bsz, SSM_HEADS, SSM_HEADDIM, SSM_STATE))


def mla_attend_prompt(q_nope, q_rope, ckv, kr, w_ukv):
    bsz, S = q_nope.shape[0], q_nope.shape[1]
    kv = (ckv @ w_ukv).reshape(bsz, S, MLA_HEADS, MLA_NOPE + MLA_V)
    k_nope, v = kv[..., :MLA_NOPE], kv[..., MLA_NOPE:]
    qb = math.gcd(S, Q_BLOCK)
    nb = S // qb
    kpos = jnp.arange(S)

    def block(i):
        start = i * qb
        qn = lax.dynamic_slice_in_dim(q_nope, start, qb, axis=1)
        qr = lax.dynamic_slice_in_dim(q_rope, start, qb, axis=1)
        s = (jnp.einsum('bqhd,bkhd->bhqk', qn, k_nope)
             + jnp.einsum('bqhr,bkr->bhqk', qr, kr)).astype(jnp.float32) * MLA_SCALE
        qpos = start + jnp.arange(qb)
        s = jnp.where(kpos[None, :] <= qpos[:, None], s, -jnp.inf)
        p = jax.nn.softmax(s, axis=-1).astype(v.dtype)
        return jnp.einsum('bhqk,bkhv->bqhv', p, v)

    out = lax.map(block, jnp.arange(nb))
    return out.swapaxes(0, 1).reshape(bsz, S, MLA_HEADS, MLA_V)


def mla_attend_paged(q_nope, q_rope, ckv, kr, w_ukv, cache_mla, page_table):
    bsz, T = q_nope.shape[0], q_nope.shape[1]
    f32 = jnp.float32
    w = w_ukv.reshape(MLA_KV_RANK, MLA_HEADS, MLA_NOPE + MLA_V)
    w_uk, w_uv = w[..., :MLA_NOPE], w[..., MLA_NOPE:]
    q_lat = jnp.einsum('bthd,chd->bhtc', q_nope, w_uk).astype(f32)
    q_r = q_rope.transpose(0, 2, 1, 3).astype(f32)

    def scores(lat, rk):
        return (jnp.einsum('bhtc,bpc->bhtp', q_lat, lat) + jnp.einsum('bhtr,bpr->bhtp', q_r, rk)) * MLA_SCALE

    def accumulate(carry, s, lat):
        m, l, acc = carry
        m_new = jnp.maximum(m, jnp.max(s, axis=-1))
        corr = jnp.exp(m - m_new)
        p = jnp.exp(s - m_new[..., None])
        return (m_new, l * corr + jnp.sum(p, axis=-1),
                acc * corr[..., None] + jnp.einsum('bhtp,bpc->bhtc', p, lat))

    def page_step(carry, phys):
        page = cache_mla[phys].astype(f32)
        lat, rk = page[..., :MLA_KV_RANK], page[..., MLA_KV_RANK:]
        return accumulate(carry, scores(lat, rk), lat), None

    init = (jnp.full((bsz, MLA_HEADS, T), -jnp.inf, f32), jnp.zeros((bsz, MLA_HEADS, T), f32),
            jnp.zeros((bsz, MLA_HEADS, T, MLA_KV_RANK), f32))
    carry, _ = lax.scan(page_step, init, page_table.T)
    lat_new = ckv.astype(f32)
    causal = jnp.tril(jnp.ones((T, T), dtype=bool))
    s_new = jnp.where(causal, scores(lat_new, kr.astype(f32)), -jnp.inf)
    m, l, acc = accumulate(carry, s_new, lat_new)
    out_lat = acc / l[..., None]
    return jnp.einsum('bhtc,chv->bthv', out_lat.astype(w_uv.dtype), w_uv)


def even_mixer(h, pos, attend, mlstm_state, w_in_even, b_ig, b_fg, mlstm_norm_g, mla_q_norm_g,
               mla_kv_norm_g, w_uq, w_ukv, w_out_even):
    bsz, L = h.shape[0], h.shape[1]
    q, k, v, ig, fg, og, cq, ckv, kr = split_cols(h @ w_in_even, EVEN_SPLITS)
    q = q.reshape(bsz, L, ML_HEADS, ML_DQK) * (ML_DQK ** -0.5)
    k = k.reshape(bsz, L, ML_HEADS, ML_DQK)
    v = v.reshape(bsz, L, ML_HEADS, ML_DV)
    logi = (ig + b_ig).astype(jnp.float32)
    logf = jax.nn.log_sigmoid((fg + b_fg).astype(jnp.float32))
    hm, C, n, m = mlstm_chunkwise(q, k, v, logi, logf, *mlstm_state)
    hm = rmsnorm(hm.astype(h.dtype), mlstm_norm_g.reshape(ML_HEADS, ML_DV))
    hm = jax.nn.sigmoid(og).reshape(bsz, L, ML_HEADS, ML_DV) * hm
    qa = (rmsnorm(cq, mla_q_norm_g) @ w_uq).reshape(bsz, L, MLA_HEADS, MLA_NOPE + MLA_ROPE)
    q_nope, q_rope = qa[..., :MLA_NOPE], rope(qa[..., MLA_NOPE:], pos)
    ckv = rmsnorm(ckv, mla_kv_norm_g)
    kr = rope(kr[:, :, None, :], pos)[:, :, 0, :]
    ha = attend(q_nope, q_rope, ckv, kr, w_ukv)
    mixed = jnp.concatenate([hm.reshape(bsz, L, -1), ha.reshape(bsz, L, -1)], axis=-1) @ w_out_even
    return mixed, jnp.concatenate([ckv, kr], axis=-1), (C, n, m)


def mamba_mixer(h, conv_buf, ssm0, w_in_ssm, conv_w, conv_b, dt_bias, A_log, D_skip, ssm_norm_g, w_out_ssm):
    bsz, L = h.shape[0], h.shape[1]
    z, xbc, dt = split_cols(h @ w_in_ssm, (SSM_D_INNER, SSM_CONV_DIM, SSM_HEADS))
    full = jnp.concatenate([conv_buf.astype(xbc.dtype), xbc], axis=1)
    conv = conv_b
    for j in range(SSM_CONV):
        conv = conv + full[:, j:j + L, :] * conv_w[j]
    new_buf = full[:, L:, :]
    xbc = jax.nn.silu(conv)
    xs, Bm, Cm = split_cols(xbc, (SSM_D_INNER, SSM_GROUPS * SSM_STATE, SSM_GROUPS * SSM_STATE))
    xs = xs.reshape(bsz, L, SSM_HEADS, SSM_HEADDIM)
    Bm = Bm.reshape(bsz, L, SSM_GROUPS, SSM_STATE)
    Cm = Cm.reshape(bsz, L, SSM_GROUPS, SSM_STATE)
    dt = jax.nn.softplus((dt + dt_bias).astype(jnp.float32))
    A = -jnp.exp(A_log.astype(jnp.float32))
    y, h_new = ssd_chunked(xs, dt, A, Bm, Cm, ssm0)
    y = y.astype(h.dtype) + D_skip[:, None] * xs
    y = (y.reshape(bsz, L, SSM_D_INNER) * jax.nn.silu(z)).reshape(bsz, L, SSM_GROUPS, SSM_D_INNER // SSM_GROUPS)
    y = rmsnorm(y, ssm_norm_g.reshape(SSM_GROUPS, SSM_D_INNER // SSM_GROUPS)).reshape(bsz, L, SSM_D_INNER)
    return y @ w_out_ssm, h_new, new_buf


def setup_inputs(seed: int = 0) -> dict:
    key = jax.random.key(seed)
    k = jax.random.split(key, 32)
    f32 = jnp.float32

    def nrm(i, shape, scale):
        return jax.random.normal(k[i], shape, f32) * scale

    n_pages = PAST_LEN // PAGE_SIZE
    n_phys = (5 * DEC_BATCH * n_pages + 3) // 4
    page_table = jax.random.permutation(k[3], n_phys)[: DEC_BATCH * n_pages].reshape(DEC_BATCH, n_pages).astype(jnp.int32)
    dt0 = jnp.exp(jax.random.uniform(k[24], (SSM_HEADS,), f32, math.log(1e-3), math.log(1e-1)))
    return {
        'x_prompt': nrm(0, (BATCH, SEQ, D_MODEL), 1.0),
        'x_sample': nrm(1, (DEC_BATCH, DEC_SEQ, D_MODEL), 1.0),
        'cache_mla': nrm(2, (n_phys, PAGE_SIZE, MLA_KV_RANK + MLA_ROPE), 1.0),
        'state_mlstm_C': nrm(4, (DEC_BATCH, ML_HEADS, ML_DV, ML_DQK), 0.5),
        'state_mlstm_n': nrm(5, (DEC_BATCH, ML_HEADS, ML_DQK), 0.5),
        'state_mlstm_m': nrm(6, (DEC_BATCH, ML_HEADS), 0.5),
        'state_ssm': nrm(7, (DEC_BATCH, SSM_HEADS, SSM_HEADDIM, SSM_STATE), 0.1),
        'state_conv': nrm(8, (DEC_BATCH, SSM_CONV - 1, SSM_CONV_DIM), 1.0),
        'page_table': page_table,
        'norm_g': 1.0 + nrm(9, (DEPTH, 3, D_MODEL), 0.02),
        'ffn_w_gate': nrm(10, (DEPTH, 2, D_MODEL, D_FF), D_MODEL ** -0.5),
        'ffn_w_up': nrm(11, (DEPTH, 2, D_MODEL, D_FF), D_MODEL ** -0.5),
        'ffn_w_down': nrm(12, (DEPTH, 2, D_FF, D_MODEL), D_FF ** -0.5),
        'w_in_even': nrm(13, (D_MODEL, EVEN_IN), D_MODEL ** -0.5),
        'b_ig': nrm(14, (ML_HEADS,), 0.1),
        'b_fg': 3.0 + nrm(15, (ML_HEADS,), 0.5),
        'mlstm_norm_g': 1.0 + nrm(16, (ML_HEADS * ML_DV,), 0.02),
        'mla_q_norm_g': 1.0 + nrm(17, (MLA_Q_RANK,), 0.02),
        'mla_kv_norm_g': 1.0 + nrm(18, (MLA_KV_RANK,), 0.02),
        'w_uq': nrm(19, (MLA_Q_RANK, MLA_HEADS * (MLA_NOPE + MLA_ROPE)), MLA_Q_RANK ** -0.5),
        'w_ukv': nrm(20, (MLA_KV_RANK, MLA_HEADS * (MLA_NOPE + MLA_V)), MLA_KV_RANK ** -0.5),
        'w_out_even': nrm(21, (EVEN_OUT, D_MODEL), EVEN_OUT ** -0.5),
        'w_in_ssm': nrm(22, (D_MODEL, SSM_IN), D_MODEL ** -0.5),
        'conv_w': nrm(23, (SSM_CONV, SSM_CONV_DIM), SSM_CONV ** -0.5),
        'conv_b': nrm(25, (SSM_CONV_DIM,), 0.02),
        'dt_bias': dt0 + jnp.log(-jnp.expm1(-dt0)),
        'A_log': jnp.log(jax.random.uniform(k[26], (SSM_HEADS,), f32, 1.0, 16.0)),
        'D_skip': 1.0 + nrm(27, (SSM_HEADS,), 0.1),
        'ssm_norm_g': 1.0 + nrm(28, (SSM_D_INNER,), 0.02),
        'w_out_ssm': nrm(29, (SSM_D_INNER, D_MODEL), SSM_D_INNER ** -0.5),
        'final_norm_g': 1.0 + nrm(30, (D_MODEL,), 0.02),
    }


def reference(x_prompt, x_sample, cache_mla, state_mlstm_C, state_mlstm_n, state_mlstm_m, state_ssm,
              state_conv, page_table, norm_g, ffn_w_gate, ffn_w_up, ffn_w_down, w_in_even, b_ig, b_fg,
              mlstm_norm_g, mla_q_norm_g, mla_kv_norm_g, w_uq, w_ukv, w_out_even, w_in_ssm, conv_w,
              conv_b, dt_bias, A_log, D_skip, ssm_norm_g, w_out_ssm, final_norm_g):
    even_w = (w_in_even, b_ig, b_fg, mlstm_norm_g, mla_q_norm_g, mla_kv_norm_g, w_uq, w_ukv, w_out_even)
    odd_w = (w_in_ssm, conv_w, conv_b, dt_bias, A_log, D_skip, ssm_norm_g, w_out_ssm)

    def trunk(x, pos, attend, mlstm_state, ssm_state, conv_state):
        for layer in range(DEPTH):
            x = x + 0.5 * swiglu(rmsnorm(x, norm_g[layer, 0]), ffn_w_gate[layer, 0], ffn_w_up[layer, 0], ffn_w_down[layer, 0])
            hn = rmsnorm(x, norm_g[layer, 1])
            if layer % 2 == 0:
                mix, rows, mlstm_state = even_mixer(hn, pos, attend, mlstm_state, *even_w)
            else:
                mix, ssm_state, conv_state = mamba_mixer(hn, conv_state, ssm_state, *odd_w)
            x = x + mix
            x = x + 0.5 * swiglu(rmsnorm(x, norm_g[layer, 2]), ffn_w_gate[layer, 1], ffn_w_up[layer, 1], ffn_w_down[layer, 1])
        return rmsnorm(x, final_norm_g), rows, mlstm_state, ssm_state, conv_state

    bp, sp = x_prompt.shape[0], x_prompt.shape[1]
    f32 = jnp.float32
    mlstm0 = (jnp.zeros((bp, ML_HEADS, ML_DV, ML_DQK), f32), jnp.zeros((bp, ML_HEADS, ML_DQK), f32),
              jnp.zeros((bp, ML_HEADS), f32))
    ssm0 = jnp.zeros((bp, SSM_HEADS, SSM_HEADDIM, SSM_STATE), f32)
    conv0 = jnp.zeros((bp, SSM_CONV - 1, SSM_CONV_DIM), x_prompt.dtype)
    y_p, rows_p, (C_p, n_p, m_p), ssm_p, conv_p = trunk(x_prompt, jnp.arange(sp), mla_attend_prompt, mlstm0, ssm0, conv0)

    def attend_sample(qn, qr, ckv, kr, w):
        return mla_attend_paged(qn, qr, ckv, kr, w, cache_mla, page_table)

    pos_s = PAST_LEN + jnp.arange(x_sample.shape[1])
    y_s, rows_s, (C_s, n_s, m_s), ssm_s, conv_s = trunk(
        x_sample, pos_s, attend_sample, (state_mlstm_C, state_mlstm_n, state_mlstm_m), state_ssm, state_conv)
    return (y_p, y_s, rows_p, rows_s, C_p, C_s, n_p, n_s, m_p, m_s, ssm_p, ssm_s, conv_p, conv_s)
```

```python
import numpy as np
import concourse.bass as bass
import concourse.mybir as mybir
from concourse.bass_utils import run_bass_kernel_spmd

F32 = mybir.dt.float32
F32R = mybir.dt.float32r
USE_F32R = False
SAME_ENGINE_SYNC = True
BF16 = mybir.dt.bfloat16
I32 = mybir.dt.int32
U32 = mybir.dt.uint32
AF = mybir.ActivationFunctionType
ALU = mybir.AluOpType
AX = mybir.AxisListType

COMPUTE = ("pe", "act", "dve", "pool")
NDSEM = 24


class Buf:
    __slots__ = ("name", "last_w", "readers")

    def __init__(self, name):
        self.name = name
        self.last_w = None
        self.readers = []


class V:
    __slots__ = ("buf", "ap")

    def __init__(self, buf, ap):
        self.buf = buf
        self.ap = ap

    def __getitem__(self, idx):
        return V(self.buf, self.ap[idx])

    def re(self, pat, **kw):
        return V(self.buf, self.ap.rearrange(pat, **kw))

    def bc(self, shape):
        return V(self.buf, self.ap.broadcast_to(shape))

    def bitcast(self, dt):
        return V(self.buf, self.ap.bitcast(dt))


class Op:
    __slots__ = ("eng", "emit", "reads", "writes", "dma", "deps", "inc", "semval", "dsem", "dval", "dprev")

    def __init__(self, eng, emit, reads, writes, dma):
        self.eng = eng
        self.emit = emit
        self.reads = reads
        self.writes = writes
        self.dma = dma
        self.deps = ()
        self.inc = False
        self.semval = 0
        self.dsem = None
        self.dval = 0
        self.dprev = 0


class Prog:
    def __init__(self, nc):
        import contextlib
        self.nc = nc
        self.ops = []
        self.nbuf = 0
        self.psum_rr = 0
        self.stacks = [contextlib.ExitStack()]
        self.live = [[]]
        self.allbufs = []
        self.last_bar = None
        self.uid = 0
        t = self.stacks[0].enter_context(self.nc.sbuf_tensor("bar_scr", [1, 8], F32))
        self.bar_t = V(self._newbuf("bar_scr"), t[:])

    def _newbuf(self, name):
        b = Buf(name)
        b.last_w = self.last_bar
        self.allbufs.append(b)
        return b

    def sbuf(self, name, shape, dt):
        self.uid += 1
        nm = "%s_u%d" % (name, self.uid)
        t = self.stacks[-1].enter_context(self.nc.sbuf_tensor(nm, list(shape), dt))
        return V(self._newbuf(nm), t[:])

    def push(self):
        import contextlib
        self.stacks.append(contextlib.ExitStack())

    def pop(self):
        self.barrier()
        self.stacks.pop().close()

    def barrier(self):
        bt = self.bar_t
        op = Op("dve", lambda e: e.memset(bt.ap, 0.0), [], list(self.allbufs), False)
        self.ops.append(op)
        self.last_bar = len(self.ops) - 1

    def psum(self, name, shape, dt):
        t = self.stacks[0].enter_context(self.nc.psum_tensor(name, list(shape), dt))
        return V(self._newbuf(name), t[:])

    def dram(self, name, shape, dt, kind="Internal"):
        t = self.nc.dram_tensor(name, list(shape), dt, kind=kind)
        return V(self._newbuf(name), t.ap())

    def add(self, eng, emit, reads, writes, dma=False):
        rb = []
        for v in reads:
            if isinstance(v, V) and v.buf not in rb:
                rb.append(v.buf)
        wb = []
        for v in writes:
            if isinstance(v, V) and v.buf not in wb:
                wb.append(v.buf)
        self.ops.append(Op(eng, emit, rb, wb, dma))

    def dma(self, out, in_, eng="sp", **kw):
        self.add(eng, lambda e: e.dma_start(out=out.ap, in_=in_.ap, **kw), [in_], [out], dma=True)

    def mm(self, out, lhsT, rhs, start=True, stop=True, **kw):
        la, ra = lhsT.ap, rhs.ap
        if USE_F32R and la.dtype == F32 and ra.dtype == F32:
            la = la.bitcast(F32R)
            ra = ra.bitcast(F32R)
        self.add("pe", lambda e: e.matmul(out.ap, la, ra, start=start, stop=stop, **kw),
                 [lhsT, rhs] + ([] if start else [out]), [out])

    def tr(self, out, in_, ident):
        self.add("pe", lambda e: e.transpose(out.ap, in_.ap, ident.ap), [in_, ident], [out])

    def act(self, out, in_, func, bias=None, scale=None, accum_out=None, eng="act"):
        kw = {}
        rd = [in_]
        wr = [out]
        if bias is not None:
            kw["bias"] = bias.ap if isinstance(bias, V) else bias
            rd.append(bias)
        if scale is not None:
            kw["scale"] = scale.ap if isinstance(scale, V) else scale
            rd.append(scale)
        if accum_out is not None:
            kw["accum_out"] = accum_out.ap
            wr.append(accum_out)
        self.add(eng, lambda e: e.activation(out.ap, in_.ap, func, **kw), rd, wr)

    def tt(self, out, in0, in1, op, eng="dve"):
        self.add(eng, lambda e: e.tensor_tensor(out.ap, in0.ap, in1.ap, op), [in0, in1], [out])

    def ts(self, out, in0, s1, s2, op0, op1=None, accum_out=None, eng="dve"):
        a1 = s1.ap if isinstance(s1, V) else s1
        a2 = s2.ap if isinstance(s2, V) else s2
        kw = {}
        wr = [out]
        if op1 is not None:
            kw["op1"] = op1
        if accum_out is not None:
            kw["accum_out"] = accum_out.ap
            wr.append(accum_out)
        self.add(eng, lambda e: e.tensor_scalar(out.ap, in0.ap, a1, a2, op0, **kw), [in0, s1, s2], wr)

    def stt(self, out, in0, scalar, in1, op0, op1, accum_out=None, eng="dve"):
        sc = scalar.ap if isinstance(scalar, V) else scalar
        kw = {}
        wr = [out]
        if accum_out is not None:
            kw["accum_out"] = accum_out.ap
            wr.append(accum_out)
        self.add(eng, lambda e: e.scalar_tensor_tensor(out.ap, in0.ap, sc, in1.ap, op0, op1, **kw),
                 [in0, scalar, in1], wr)

    def copy(self, out, in_, eng="dve"):
        self.add(eng, lambda e: e.tensor_copy(out.ap, in_.ap), [in_], [out])

    def memset(self, out, val, eng="dve"):
        self.add(eng, lambda e: e.memset(out.ap, val), [], [out])

    def reduce(self, out, in_, op, axis=AX.X, eng="dve"):
        self.add(eng, lambda e: e.tensor_reduce(out.ap, in_.ap, axis, op), [in_], [out])

    def recip(self, out, in_):
        self.add("dve", lambda e: e.reciprocal(out.ap, in_.ap), [in_], [out])

    def generic(self, eng, fn, reads, writes, dma=False):
        self.add(eng, fn, reads, writes, dma=dma)

    def finish(self, block_ctx):
        nc = self.nc
        ops = self.ops
        for i, op in enumerate(ops):
            deps = set()
            for b in op.reads:
                if b.last_w is not None:
                    deps.add(b.last_w)
            for b in op.writes:
                if b.last_w is not None:
                    deps.add(b.last_w)
                for r in b.readers:
                    deps.add(r)
            deps.discard(i)
            for b in op.reads:
                if not op.dma:
                    b.readers = [r for r in b.readers if ops[r].dma or ops[r].eng != op.eng]
                b.readers.append(i)
            for b in op.writes:
                b.last_w = i
                b.readers = []
            keep = []
            for j in deps:
                pj = ops[j]
                if pj.dma:
                    keep.append(j)
                elif op.dma:
                    keep.append(j)
                elif pj.eng != op.eng:
                    keep.append(j)
                elif op.eng == "pool" or (SAME_ENGINE_SYNC and op.eng != "pe"):
                    if any(b in pj.writes for b in op.reads + op.writes):
                        keep.append(j)
            op.deps = keep
            for j in keep:
                if not ops[j].dma:
                    ops[j].inc = True
        cnt = {e: 0 for e in COMPUTE}
        dq = {}
        for op in ops:
            if op.dma:
                q = dq.setdefault(op.eng, [0, [0] * NDSEM])
                k = q[0] % NDSEM
                q[0] += 1
                op.dsem = (op.eng, k)
                op.dprev = q[1][k]
                q[1][k] += 16
                op.dval = q[1][k]
            elif op.inc:
                cnt[op.eng] += 1
                op.semval = cnt[op.eng]
        self.stats = dict(cnt=cnt, n=len(ops), ndma={e: dq[e][0] for e in dq})
        sems = {}
        import contextlib
        st = contextlib.ExitStack()
        for e in COMPUTE:
            sems[e] = st.enter_context(nc.semaphore("s_" + e))
        for e in dq:
            for k in range(NDSEM):
                sems[(e, k)] = st.enter_context(nc.semaphore("d_%s_%d" % (e, k)))
        per = {}
        for i, op in enumerate(ops):
            per.setdefault(op.eng, []).append(i)
        final_d = {(e, k): dq[e][1][k] for e in dq for k in range(NDSEM)}

        def run(engname, eng):
            waited = {}

            def wait(key, val):
                if val <= 0 or waited.get(key, 0) >= val:
                    return
                eng.wait_ge(sems[key], val)
                waited[key] = val

            for i in per.get(engname, []):
                op = ops[i]
                need = {}
                for j in op.deps:
                    pj = ops[j]
                    if pj.dma:
                        key, val = pj.dsem, pj.dval
                    else:
                        key, val = pj.eng, pj.semval
                    if need.get(key, 0) < val:
                        need[key] = val
                if op.dma and op.dprev > 0:
                    if need.get(op.dsem, 0) < op.dprev:
                        need[op.dsem] = op.dprev
                for key, val in need.items():
                    wait(key, val)
                ins = op.emit(eng)
                if op.dma:
                    ins.then_inc(sems[op.dsem], 16)
                elif op.inc:
                    ins.then_inc(sems[op.eng], 1)
            if engname == "sp":
                for key, val in final_d.items():
                    wait(key, val)

        with nc.Block() as block:
            @block.sync
            def _(e):
                run("sp", e)

            @block.tensor
            def _(e):
                run("pe", e)

            @block.scalar
            def _(e):
                run("act", e)

            @block.vector
            def _(e):
                run("dve", e)

            @block.gpsimd
            def _(e):
                run("pool", e)
        st.close()


EPS = 1e-6


class Ctx:
    pass


def tok_tiles(T, tw=512):
    out = []
    t = 0
    while t < T:
        w = min(tw, T - t)
        out.append((t, w))
        t += w
    return out


def setup_common(P, c):
    c.ones_bf = P.sbuf("ones_bf", [128, 128], BF16)
    P.memset(c.ones_bf, 1.0)
    c.ps = [P.psum("ps%d" % i, [128, 512], F32) for i in range(6)]
    c.ps_i = 0
    c.psb = [P.psum("psb%d" % i, [128, 1024], BF16) for i in range(2)]
    c.psb_i = 0
    c.wrr = {}


def next_ps(c):
    v = c.ps[c.ps_i % 6]
    c.ps_i += 1
    return v


def next_psb(c):
    k = c.psb_i % 4
    c.psb_i += 1
    return c.psb[k // 2][:, (k % 2) * 512:(k % 2 + 1) * 512]


def rr(c, P, name, n, shape, dt):
    if name not in c.wrr:
        c.wrr[name] = [[P.sbuf("%s_%d" % (name, i), shape, dt) for i in range(n)], 0]
    ent = c.wrr[name]
    v = ent[0][ent[1] % n]
    ent[1] += 1
    return v


def norm_stage(P, c, src, g_dram, dst, D, T):
    P.push()
    c.wrr = {}
    KC = D // 128
    g_sb = load_vec_fm(P, "g_n", g_dram, D)
    for (t0, tw) in tok_tiles(T, 256):
        xo = rr(c, P, "nxo", 2, [128, KC, 256], BF16)
        norm_fm(P, c, src, g_sb, xo, D, t0, tw)
        P.dma(dst.re("(k p) t -> p k t", p=128)[:, :, t0:t0 + tw], xo[:, :, :tw], eng="pool")
    P.pop()


def norm_fm(P, c, src, g_sb, dst_bf, D, t0, tw, dst_off=0):
    KC = D // 128
    xin = rr(c, P, "nx", 2, [128, KC, 256], F32)
    sq = rr(c, P, "nsq", 2, [128, KC, 256], BF16)
    P.dma(xin[:, :, :tw], src.re("(k p) t -> p k t", p=128)[:, :, t0:t0 + tw])
    P.act(sq[:, :, :tw], xin[:, :, :tw], AF.Square)
    ps = next_ps(c)
    for kc in range(KC):
        P.mm(ps[:, :tw], c.ones_bf, sq[:, kc, :tw], start=(kc == 0), stop=(kc == KC - 1))
    rstd = rr(c, P, "nrs", 2, [128, 512], F32)
    P.ts(rstd[:, :tw], ps[:, :tw], 1.0 / D, EPS, ALU.mult, ALU.add)
    P.act(rstd[:, :tw], rstd[:, :tw], AF.Sqrt)
    P.recip(rstd[:, :tw], rstd[:, :tw])
    for kc in range(KC):
        P.stt(dst_bf[:, kc, dst_off:dst_off + tw], xin[:, kc, :tw], g_sb[:, kc:kc + 1], rstd[:, :tw],
              ALU.mult, ALU.mult)
    return xin


def load_vec_fm(P, name, vec_dram, D):
    KC = D // 128
    t = P.sbuf(name, [128, KC], F32)
    P.dma(t, vec_dram.re("(k p) -> p k", p=128), allow_slow_non_contiguous=True)
    return t


def ffn_stage(P, c, src, dst, xn_scr, g_dram, wg, wu, wd, D, FF, T, blocks):
    norm_stage(P, c, src, g_dram, xn_scr, D, T)
    P.push()
    c.wrr = {}
    KC = D // 128
    FC = FF // 128
    TBMAX = max(b[1] for b in blocks)
    xn = rr(c, P, "xn", 1, [128, KC, TBMAX], BF16)
    hT = rr(c, P, "hT", 1, [128, FC, TBMAX], BF16)
    for (b0, bw) in blocks:
        tiles = tok_tiles(bw)
        P.dma(xn[:, :, :bw], xn_scr.re("(k p) t -> p k t", p=128)[:, :, b0:b0 + bw])
        for fc in range(FC):
            wgb = rr(c, P, "wg", 2, [128, KC * 128], BF16)
            wub = rr(c, P, "wu", 2, [128, KC * 128], BF16)
            P.dma(wgb, wg[fc], eng="pool")
            P.dma(wub, wu[fc], eng="pool")
            for (t0, tw) in tiles:
                pg = next_ps(c)
                pu = next_ps(c)
                for kc in range(KC):
                    P.mm(pg[:, :tw], wgb[:, kc * 128:(kc + 1) * 128], xn[:, kc, t0:t0 + tw],
                         start=(kc == 0), stop=(kc == KC - 1))
                for kc in range(KC):
                    P.mm(pu[:, :tw], wub[:, kc * 128:(kc + 1) * 128], xn[:, kc, t0:t0 + tw],
                         start=(kc == 0), stop=(kc == KC - 1))
                sg = rr(c, P, "sg", 3, [128, 512], F32)
                P.act(sg[:, :tw], pg[:, :tw], AF.Silu)
                P.tt(hT[:, fc, t0:t0 + tw], sg[:, :tw], pu[:, :tw], ALU.mult)
        for dc in range(KC):
            wdb = rr(c, P, "wd", 2, [128, FC * 128], BF16)
            P.dma(wdb, wd[dc], eng="pool")
            for (t0, tw) in tiles:
                xr = rr(c, P, "xr", 3, [128, 512], F32)
                P.dma(xr[:, :tw], src[dc * 128:(dc + 1) * 128, b0 + t0:b0 + t0 + tw])
                py = next_ps(c)
                for fc in range(FC):
                    P.mm(py[:, :tw], wdb[:, fc * 128:(fc + 1) * 128], hT[:, fc, t0:t0 + tw],
                         start=(fc == 0), stop=(fc == FC - 1))
                xo = rr(c, P, "xo", 3, [128, 512], F32)
                P.stt(xo[:, :tw], py[:, :tw], 0.5, xr[:, :tw], ALU.mult, ALU.add)
                P.dma(dst[dc * 128:(dc + 1) * 128, b0 + t0:b0 + t0 + tw], xo[:, :tw])
    P.pop()


def tile_w(W, pad_to=128):
    K, N = W.shape
    NP = (N + pad_to - 1) // pad_to * pad_to
    if NP != N:
        W = np.concatenate([W, np.zeros((K, NP - N), W.dtype)], axis=1)
    KC = K // 128
    NB = NP // 128
    return np.ascontiguousarray(W.reshape(KC, 128, NB, 128).transpose(2, 1, 0, 3).reshape(NB, 128, KC * 128))


DEBUG = False
D_MODEL = 2048
D_FF = 5632
SEQ = 2048
NS = 16
LS = 4
TS = NS * LS
T_ALL = SEQ + TS
PAST = 8192
EVEN_IN = 4168
C_Q, C_K, C_V, C_IG, C_FG, C_OG, C_CQ, C_CKV, C_KR = 0, 512, 1024, 2048, 2052, 2056, 3080, 3592, 4104
BLOCKS = [(0, 1024), (1024, 1088)]


def inproj_tok(P, c, xn, W, K, N, T, z, evac_i=[0]):
    KC = K // 128
    Wv = W.re("(k p) n -> p k n", p=128)
    for n0 in range(0, N, 512):
        nw = min(512, N - n0)
        wb = rr(c, P, "wi", 2, [128, KC, 512], BF16)
        P.dma(wb[:, :, :nw], Wv[:, :, n0:n0 + nw], eng="pool")
        for t0 in range(0, T, 128):
            tw = min(128, T - t0)
            ps = next_ps(c)
            for kc in range(KC):
                P.mm(ps[:tw, :nw], xn[:, kc, t0:t0 + tw], wb[:, kc, :nw], start=(kc == 0), stop=(kc == KC - 1))
            zo = rr(c, P, "zo", 3, [128, 512], F32)
            evac_i[0] += 1
            if evac_i[0] % 2:
                P.copy(zo[:tw, :nw], ps[:tw, :nw], eng="dve")
            else:
                P.act(zo[:tw, :nw], ps[:tw, :nw], AF.Copy)
            P.dma(z[t0:t0 + tw, n0:n0 + nw], zo[:tw, :nw])


def inproj_fm(P, c, xn, W, K, segs, T, dst, scale=None, odt=None):
    KC = K // 128
    Wv = W.re("(k p) n -> p k n", p=128)
    M = sum(n for _, n in segs)
    wb = rr(c, P, "wf%d" % KC, 2, [128, KC, 128], BF16)
    o = 0
    for (c0, n) in segs:
        P.dma(wb[:, :, o:o + n], Wv[:, :, c0:c0 + n], eng="pool")
        o += n
    for (t0, tw) in tok_tiles(T):
        ps = next_ps(c)
        for kc in range(KC):
            P.mm(ps[:M, :tw], wb[:, kc, :M], xn[:, kc, t0:t0 + tw], start=(kc == 0), stop=(kc == KC - 1))
        zo = rr(c, P, "zf" + ("b" if odt == BF16 else ""), 3, [128, 512], odt or F32)
        if scale is not None:
            P.act(zo[:M, :tw], ps[:M, :tw], AF.Copy, scale=scale)
        else:
            P.copy(zo[:M, :tw], ps[:M, :tw])
        P.dma(dst[:, t0:t0 + tw], zo[:M, :tw])


def bcast_rows(P, name, vec_dram, n, parts=128):
    t = P.sbuf(name, [parts, n], F32)
    P.dma(t, vec_dram.re("(o n) -> o n", o=1).bc([parts, n]), allow_slow_non_contiguous=True)
    return t


def rows_stage(P, c, z, kvg_dram, cs_dram, rows_out, T):
    g_bc = bcast_rows(P, "kvg_bc", kvg_dram, 512)
    for t0 in range(0, T, 128):
        tw = min(128, T - t0)
        zi = rr(c, P, "rz", 2, [128, 576], F32)
        P.dma(zi[:tw], z[t0:t0 + tw, C_CKV:C_CKV + 576])
        cs = rr(c, P, "rcs", 2, [128, 64], F32)
        P.dma(cs[:tw], cs_dram[t0:t0 + tw, :])
        junk = rr(c, P, "rjunk", 2, [128, 512], F32)
        ss = rr(c, P, "rss", 2, [128, 1], F32)
        P.memset(ss[:tw], 0.0)
        P.act(junk[:tw], zi[:tw, 0:512], AF.Square, accum_out=ss[:tw])
        P.ts(ss[:tw], ss[:tw], 1.0 / 512, EPS, ALU.mult, ALU.add)
        P.act(ss[:tw], ss[:tw], AF.Sqrt)
        P.recip(ss[:tw], ss[:tw])
        ro = rr(c, P, "ro", 2, [128, 576], F32)
        P.stt(ro[:tw, 0:512], zi[:tw, 0:512], ss[:tw, 0:1], g_bc[:tw], ALU.mult, ALU.mult)
        rope_tok(P, c, ro[:tw, 512:576], zi[:tw, 512:576], cs[:tw], tw)
        P.dma(rows_out[t0:t0 + tw, :], ro[:tw], eng="pool")


def rope_tok(P, c, out, x, cs, tw):
    t1 = rr(c, P, "rp1", 2, [128, 32], F32)
    t2 = rr(c, P, "rp2", 2, [128, 32], F32)
    x1, x2 = x[:, 0:32], x[:, 32:64]
    co, si = cs[:, 0:32], cs[:, 32:64]
    P.tt(t1[:tw], x1, co, ALU.mult)
    P.tt(t2[:tw], x2, si, ALU.mult)
    P.tt(out[:, 0:32], t1[:tw], t2[:tw], ALU.subtract)
    P.tt(t1[:tw], x1, si, ALU.mult)
    P.tt(t2[:tw], x2, co, ALU.mult)
    P.tt(out[:, 32:64], t1[:tw], t2[:tw], ALU.add)


def gates_fm(P, c, xn, W, T):
    KC = D_MODEL // 128
    wg8 = P.sbuf("wgate8", [128, KC, 8], BF16)
    P.dma(wg8, W.re("(k p) n -> p k n", p=128)[:, :, C_IG:C_IG + 8], eng="pool")
    igT = P.sbuf("igT", [4, T], F32)
    fgT = P.sbuf("fgT", [4, T], F32)
    for (t0, tw) in tok_tiles(T):
        for j, dst in ((0, igT), (1, fgT)):
            ps = next_ps(c)
            for kc in range(KC):
                P.mm(ps[:4, :tw], wg8[:, kc, 4 * j:4 * j + 4], xn[:, kc, t0:t0 + tw],
                     start=(kc == 0), stop=(kc == KC - 1))
            P.copy(dst[:, t0:t0 + tw], ps[:4, :tw])
    return igT, fgT


def hs_scan(P, S, src3, lc, op, name):
    nch = src3.ap.shape[1]
    cur = src3
    bufs = [S(name + "_a", [4, nch * lc]).re("h (c s) -> h c s", s=lc),
            S(name + "_b", [4, nch * lc]).re("h (c s) -> h c s", s=lc)]
    k = 0
    sh = 1
    while sh < lc:
        oth = bufs[k % 2]
        k += 1
        P.copy(oth[:, :, :sh], cur[:, :, :sh])
        P.tt(oth[:, :, sh:], cur[:, :, sh:], cur[:, :, :lc - sh], op)
        cur = oth
        sh *= 2
    return cur


def mlstm_gate_math(P, c, tag, igT, fgT, big_col, nbfg_col, t0, L, lc, m0, sequential):
    nch = L // lc
    S = lambda n, shape: P.sbuf("%s_%s" % (tag, n), shape, F32)
    e = S("e", [4, L])
    P.act(e, fgT[:, t0:t0 + L], AF.Exp, bias=nbfg_col, scale=-1.0)
    P.ts(e, e, 1.0, None, ALU.add)
    P.act(e, e, AF.Ln)
    P.ts(e, e, -1.0, None, ALU.mult)
    b3 = hs_scan(P, S, e.re("h (c s) -> h c s", s=lc), lc, ALU.add, "cs")
    a = S("a", [4, L])
    a3 = a.re("h (c s) -> h c s", s=lc)
    P.stt(a3, igT[:, t0:t0 + L].re("h (c s) -> h c s", s=lc), big_col, b3, ALU.add, ALU.subtract)
    At3 = hs_scan(P, S, a3, lc, ALU.max, "cm")
    amax = At3[:, :, lc - 1]
    blast = b3[:, :, lc - 1]
    M = S("M", [4, nch])
    mprev = S("mprev", [4, nch + 1])
    if sequential:
        if m0 is None:
            P.memset(mprev[:, 0:1], 0.0)
        else:
            P.copy(mprev[:, 0:1], m0)
        for ch in range(nch):
            P.tt(M[:, ch:ch + 1], mprev[:, ch:ch + 1], amax[:, ch:ch + 1], ALU.max)
            P.tt(mprev[:, ch + 1:ch + 2], M[:, ch:ch + 1], blast[:, ch:ch + 1], ALU.add)
        mnew = mprev[:, 1:nch + 1]
    else:
        P.copy(mprev[:, 0:nch], m0)
        P.tt(M, mprev[:, 0:nch], amax, ALU.max)
        mnew = S("mnew", [4, nch])
        P.tt(mnew, M, blast, ALU.add)
    wgT = S("wgT", [4, L])
    w3 = wgT.re("h (c s) -> h c s", s=lc)
    Mt = S("Mt", [4, L])
    Mt3 = Mt.re("h (c s) -> h c s", s=lc)
    win = S("win", [4, L])
    win3 = win.re("h (c s) -> h c s", s=lc)
    for ch in range(nch):
        P.ts(w3[:, ch, :], a3[:, ch, :], M[:, ch:ch + 1], None, ALU.subtract)
        P.ts(Mt3[:, ch, :], At3[:, ch, :], mprev[:, ch:ch + 1], None, ALU.max)
        P.ts(win3[:, ch, :], Mt3[:, ch, :], mprev[:, ch:ch + 1], None, ALU.subtract)
    P.act(wgT, wgT, AF.Exp)
    P.act(win, win, AF.Exp, scale=-1.0)
    em = S("em", [4, L])
    P.tt(em.re("h (c s) -> h c s", s=lc), b3, Mt3, ALU.add)
    P.act(em, em, AF.Exp, scale=-1.0)
    decay = S("decay", [4, nch])
    P.tt(decay, mprev[:, 0:nch], M, ALU.subtract)
    P.act(decay, decay, AF.Exp)
    return dict(wg=wgT, decay=decay, mnew=mnew, a=a, Mt=Mt, win=win, em=em)


def to_tok(P, name, srcT, scr, L, lc):
    nch = L // lc
    P.dma(scr, srcT)
    t = P.sbuf(name, [lc, nch, 4], F32)
    for h in range(4):
        P.dma(t[:, :, h], scr[h].re("(c s) -> s c", s=lc), allow_slow_non_contiguous=True)
    return t


def mlstm_unit(P, c, ch, h, qT, kT, ve, Mrow, a_t, win_t, em_t, maskT, inter_fn, hr):
    ps_s = next_ps(c)
    P.mm(ps_s[:64, :64], kT[:, h, :], qT[:, h, :])
    arg = rr(c, P, "m_arg", 2, [64, 64], F32)
    P.ts(arg, Mrow[:, h, :], a_t[:, ch, h:h + 1], 0.0, ALU.subtract, ALU.max)
    P.act(arg, arg, AF.Exp, scale=-1.0)
    P.tt(arg, arg, maskT, ALU.mult)
    sp = rr(c, P, "m_sp", 2, [64, 64], F32)
    P.tt(sp, ps_s[:64, :64], arg, ALU.mult)
    ps_n = next_ps(c)
    P.mm(ps_n[:64, :257], sp, ve[:, h, :])
    ps_i = inter_fn(h)
    ndn = rr(c, P, "m_ndn", 2, [64, 257], F32)
    P.act(ndn, ps_n[:64, :257], AF.Copy)
    nd = rr(c, P, "m_nd", 2, [64, 257], F32)
    P.stt(nd, ps_i[:64, :257], win_t[:, ch, h:h + 1], ndn, ALU.mult, ALU.add)
    dn = rr(c, P, "m_dn", 2, [64, 1], F32)
    P.act(dn, nd[:, 256:257], AF.Abs)
    P.ts(dn, dn, em_t[:, ch, h:h + 1], None, ALU.max)
    P.recip(dn, dn)
    P.ts(hr[:, h * 256:(h + 1) * 256], nd[:, 0:256], dn[:, 0:1], None, ALU.mult)


def mlstm_finish_h(P, c, hr, og_src, g_bc, hmix_dst):
    og = rr(c, P, "m_og", 2, [64, 1024], F32)
    P.dma(og, og_src)
    P.act(og, og, AF.Sigmoid)
    ss = rr(c, P, "m_ss", 2, [64, 4], F32)
    junk = rr(c, P, "m_junk", 1, [64, 256], F32)
    P.memset(ss, 0.0)
    for h in range(4):
        P.act(junk, hr[:, h * 256:(h + 1) * 256], AF.Square, accum_out=ss[:, h:h + 1])
    P.ts(ss, ss, 1.0 / 256, EPS, ALU.mult, ALU.add)
    P.act(ss, ss, AF.Sqrt)
    P.recip(ss, ss)
    ho = rr(c, P, "m_ho", 2, [64, 1024], F32)
    for h in range(4):
        P.stt(ho[:, h * 256:(h + 1) * 256], hr[:, h * 256:(h + 1) * 256], ss[:, h:h + 1],
              g_bc[:64, h * 256:(h + 1) * 256], ALU.mult, ALU.mult)
    P.tt(ho, ho, og, ALU.mult)
    P.dma(hmix_dst, ho, eng="pool")


def mlstm_state_stage(P, c, z, qkT, igT, fgT, big_d, bfg_d, mg_d, ct0_d, m0s_d, sel_d, maskp_d, masks_d, selT_d,
                      hmix, ctp_out, cts_out, mp_out, ms_out):
    big_col = P.sbuf("big_col", [4, 1], F32)
    nbfg_col = P.sbuf("nbfg_col", [4, 1], F32)
    P.dma(big_col, big_d.re("(h o) -> h o", o=1), allow_slow_non_contiguous=True)
    P.dma(nbfg_col, bfg_d.re("(h o) -> h o", o=1), allow_slow_non_contiguous=True)
    P.ts(nbfg_col, nbfg_col, -1.0, None, ALU.mult)
    m0s = P.sbuf("m0s", [4, NS], F32)
    P.dma(m0s, m0s_d)
    g_bc = bcast_rows(P, "mg_bc", mg_d, 1024, parts=64)
    maskp = P.sbuf("maskp", [64, 64], F32)
    P.dma(maskp, maskp_d)
    masks = P.sbuf("masks", [64, 64], F32)
    P.dma(masks, masks_d)
    scr = [P.dram("gscr%d" % i, [4, SEQ], F32) for i in range(5)]
    scr_s = [P.dram("gscrs%d" % i, [4, TS], F32) for i in range(5)]
    NCH = SEQ // 64
    P.push()
    G = mlstm_gate_math(P, c, "gp", igT, fgT, big_col, nbfg_col, 0, SEQ, 64, None, True)
    dec_d = P.dram("dec_scr_p", [4, NCH], F32)
    P.dma(dec_d, G["decay"])
    P.dma(mp_out, G["mnew"][:, NCH - 1:NCH])
    P.dma(scr[4], G["Mt"])
    P.dma(scr[0], G["wg"]); P.dma(scr[1], G["a"]); P.dma(scr[2], G["win"]); P.dma(scr[3], G["em"])
    P.pop()
    P.push()
    c.wrr = {}
    toks = []
    for i in range(4):
        t = P.sbuf("gtok%d" % i, [64, NCH, 4], F32)
        for h in range(4):
            P.dma(t[:, :, h], scr[i][h].re("(c s) -> s c", s=64), allow_slow_non_contiguous=True)
        toks.append(t)
    wg_tok, a_tok, win_tok, em_tok = toks
    dec_bc = P.sbuf("dec_bc_p", [128, 4 * NCH], F32)
    P.dma(dec_bc, dec_d.re("h c -> (h c)").re("(o n) -> o n", o=1).bc([128, 4 * NCH]), allow_slow_non_contiguous=True)
    Cst = [P.sbuf("Cst%d" % h, [128, 257], F32) for h in range(4)]
    for h in range(4):
        P.memset(Cst[h], 0.0)
    vext = [P.sbuf("vext%d" % i, [64, 4, 257], F32) for i in range(2)]
    for i in range(2):
        P.memset(vext[i][:, :, 256:257], 1.0)
    for ch in range(NCH):
        tsl = slice(ch * 64, (ch + 1) * 64)
        kvt = rr(c, P, "kvt", 2, [64, 1536], F32)
        P.dma(kvt, z[tsl, C_K:C_K + 1536])
        ve = vext[ch % 2]
        P.copy(ve[:, :, 0:256], kvt[:, 512:1536].re("s (h v) -> s h v", h=4), eng="pool")
        qT = rr(c, P, "m_qT", 2, [128, 4, 64], F32)
        kT = rr(c, P, "m_kT", 2, [128, 4, 64], F32)
        P.dma(qT, qkT[0:512, tsl].re("(h d) t -> d h t", h=4))
        P.dma(kT, qkT[512:1024, tsl].re("(h d) t -> d h t", h=4))
        Mrow = rr(c, P, "m_Mrow", 2, [64, 4, 64], F32)
        P.dma(Mrow, scr[4][:, tsl].re("(o h) t -> o h t", o=1).bc([64, 4, 64]), allow_slow_non_contiguous=True)
        hr = rr(c, P, "m_hr", 2, [64, 1024], F32)

        def inter_fn(h, qT=qT):
            ps_i = next_ps(c)
            P.mm(ps_i[:64, :257], qT[:, h, :], Cst[h])
            return ps_i
        for h in range(4):
            mlstm_unit(P, c, ch, h, qT, kT, ve, Mrow, a_tok, win_tok, em_tok, maskp, inter_fn, hr)
        mlstm_finish_h(P, c, hr, z[tsl, C_OG:C_OG + 1024], g_bc, hmix[tsl, 0:1024])
        for h in range(4):
            kw = rr(c, P, "kw", 3, [64, 128], F32)
            P.ts(kw, kvt[:, h * 128:(h + 1) * 128], wg_tok[:, ch, h:h + 1], None, ALU.mult)
            ps = next_ps(c)
            P.mm(ps[:, :257], kw, ve[:, h, :])
            P.stt(Cst[h], Cst[h], dec_bc[:, h * NCH + ch:h * NCH + ch + 1], ps[:, :257], ALU.mult, ALU.add)
    for h in range(4):
        P.dma(ctp_out[h], Cst[h])
    P.pop()
    P.push()
    c.wrr = {}
    G = mlstm_gate_math(P, c, "gs", igT, fgT, big_col, nbfg_col, SEQ, TS, LS, m0s, False)
    P.dma(ms_out, G["mnew"])
    dec_ds = P.dram("dec_scr_s", [4, NS], F32)
    P.dma(dec_ds, G["decay"])
    P.dma(scr_s[4], G["Mt"])
    toks = []
    for i, k in enumerate(["wg", "a", "win", "em"]):
        P.dma(scr_s[i], G[k])
        t = P.sbuf("gtoks%d" % i, [64, 1, 4], F32)
        P.dma(t[:, 0, :], scr_s[i].re("h s -> s h"), allow_slow_non_contiguous=True)
        toks.append(t)
    wg_toks, a_toks, win_toks, em_toks = toks
    dec_bcs = P.sbuf("dec_bc_s", [128, 4 * NS], F32)
    P.dma(dec_bcs, dec_ds.re("h c -> (h c)").re("(o n) -> o n", o=1).bc([128, 4 * NS]), allow_slow_non_contiguous=True)
    sel = P.sbuf("sel_s", [64, NS], F32)
    P.dma(sel, sel_d)
    selT = P.sbuf("selT_s", [128, NS, 64], F32)
    P.dma(selT, selT_d)
    tsl = slice(SEQ, SEQ + TS)
    kvs = P.sbuf("kvs", [64, 1536], F32)
    P.dma(kvs, z[tsl, C_K:C_K + 1536])
    ves = P.sbuf("vexts", [64, 4, 257], F32)
    P.memset(ves[:, :, 256:257], 1.0)
    P.copy(ves[:, :, 0:256], kvs[:, 512:1536].re("s (h v) -> s h v", h=4))
    qT = P.sbuf("s_qT", [128, 4, 64], F32)
    kT = P.sbuf("s_kT", [128, 4, 64], F32)
    P.dma(qT, qkT[0:512, tsl].re("(h d) t -> d h t", h=4))
    P.dma(kT, qkT[512:1024, tsl].re("(h d) t -> d h t", h=4))
    Mrow = P.sbuf("s_Mrow", [64, 4, 64], F32)
    P.dma(Mrow, scr_s[4].re("(o h) t -> o h t", o=1).bc([64, 4, 64]), allow_slow_non_contiguous=True)
    hr = P.sbuf("s_hr", [64, 1024], F32)
    c0s = {}

    def inter_fn_s(h):
        qm = rr(c, P, "s_qm", 2, [128, NS, 64], F32)
        P.tt(qm, qT[:, h, :].re("d (o t) -> d o t", o=1).bc([128, NS, 64]), selT, ALU.mult)
        ps_i = next_ps(c)
        for b in range(NS):
            c0 = rr(c, P, "c0", 4, [128, 257], F32)
            P.dma(c0, ct0_d[b, h])
            P.mm(ps_i[:64, :257], qm[:, b, :], c0, start=(b == 0), stop=(b == NS - 1))
        return ps_i
    for h in range(4):
        mlstm_unit(P, c, 0, h, qT, kT, ves, Mrow, a_toks, win_toks, em_toks, masks, inter_fn_s, hr)
    mlstm_finish_h(P, c, hr, z[tsl, C_OG:C_OG + 1024], g_bc, hmix[tsl, 0:1024])
    for b in range(NS):
        wsel = rr(c, P, "wsel", 2, [64, 4], F32)
        P.ts(wsel, wg_toks[:, 0, :], sel[:, b:b + 1], None, ALU.mult)
        for h in range(4):
            kw = rr(c, P, "kw", 3, [64, 128], F32)
            P.ts(kw, kvs[:, h * 128:(h + 1) * 128], wsel[:, h:h + 1], None, ALU.mult)
            ps = next_ps(c)
            P.mm(ps[:, :257], kw, ves[:, h, :])
            c0 = rr(c, P, "c0", 4, [128, 257], F32)
            P.dma(c0, ct0_d[b, h])
            cn = rr(c, P, "cn", 3, [128, 257], F32)
            P.stt(cn, c0, dec_bcs[:, h * NS + b:h * NS + b + 1], ps[:, :257], ALU.mult, ALU.add)
            P.dma(cts_out[b, h], cn)
    P.pop()


def even_front_stage(P, c, xn_scr, w_in, kvg_d, cs_d, big_d, bfg_d, mg_d, ct0_d, m0s_d, sel_d, maskp_d, masks_d,
                     selT_d, z, qkT, cqT, ckvT, krT2, hmix, rows_out, ctp_out, cts_out, mp_out, ms_out):
    igT = P.sbuf("igT", [4, T_ALL], F32)
    fgT = P.sbuf("fgT", [4, T_ALL], F32)
    P.push()
    c.wrr = {}
    KC = D_MODEL // 128
    xnf = P.sbuf("xnf", [128, KC, T_ALL], BF16)
    P.dma(xnf, xn_scr.re("(k p) t -> p k t", p=128))
    inproj_tok(P, c, xnf, w_in, D_MODEL, EVEN_IN, T_ALL, z)
    gates_fm2(P, c, xnf, w_in, T_ALL, igT, fgT)
    for i in range(4):
        inproj_fm(P, c, xnf, w_in, D_MODEL, [(C_Q + i * 128, 128)], T_ALL, qkT[i * 128:(i + 1) * 128], scale=128 ** -0.5)
        inproj_fm(P, c, xnf, w_in, D_MODEL, [(C_K + i * 128, 128)], T_ALL, qkT[512 + i * 128:512 + (i + 1) * 128])
        inproj_fm(P, c, xnf, w_in, D_MODEL, [(C_CQ + i * 128, 128)], T_ALL, cqT[i * 128:(i + 1) * 128])
        inproj_fm(P, c, xnf, w_in, D_MODEL, [(C_CKV + i * 128, 128)], T_ALL, ckvT[i * 128:(i + 1) * 128])
    inproj_fm(P, c, xnf, w_in, D_MODEL, [(C_KR, 64), (C_KR + 32, 32), (C_KR, 32)], T_ALL, krT2)
    P.pop()
    P.push()
    c.wrr = {}
    rows_stage(P, c, z, kvg_d, cs_d, rows_out, T_ALL)
    P.pop()
    P.push()
    c.wrr = {}
    mlstm_state_stage(P, c, z, qkT, igT, fgT, big_d, bfg_d, mg_d, ct0_d, m0s_d, sel_d, maskp_d, masks_d, selT_d,
                      hmix, ctp_out, cts_out, mp_out, ms_out)
    P.pop()


MLA_SCALE = 192 ** -0.5
NEG = -1.0e30


def rope_fm(P, c, dst, raw_src, swap_src, cosT, sinST, T):
    for (t0, tw) in tok_tiles(T):
        a = rr(c, P, "rf_a", 2, [64, 512], F32)
        b = rr(c, P, "rf_b", 2, [64, 512], F32)
        co = rr(c, P, "rf_c", 2, [64, 512], F32)
        si = rr(c, P, "rf_s", 2, [64, 512], F32)
        P.dma(a[:, :tw], raw_src[:, t0:t0 + tw])
        P.dma(b[:, :tw], swap_src[:, t0:t0 + tw])
        P.dma(co[:, :tw], cosT[:, t0:t0 + tw])
        P.dma(si[:, :tw], sinST[:, t0:t0 + tw])
        P.tt(a[:, :tw], a[:, :tw], co[:, :tw], ALU.mult)
        P.tt(b[:, :tw], b[:, :tw], si[:, :tw], ALU.mult)
        P.tt(a[:, :tw], a[:, :tw], b[:, :tw], ALU.add)
        P.dma(dst[:, t0:t0 + tw], a[:, :tw], eng="pool")


def mla_prep_stage(P, c, cqT, ckvT, krT2, qg_d, kvg_d, w_uq, w_ukv, cosT, sinST, cqnT, ckvnT, qaT, qr2T, knT, krrT, z2):
    norm_stage(P, c, cqT, qg_d, cqnT, 512, T_ALL)
    norm_stage(P, c, ckvT, kvg_d, ckvnT, 512, T_ALL)
    P.push()
    c.wrr = {}
    cqn = P.sbuf("cqn", [128, 4, T_ALL], BF16)
    ckvn = P.sbuf("ckvn", [128, 4, T_ALL], BF16)
    P.dma(cqn, cqnT.re("(k p) t -> p k t", p=128))
    P.dma(ckvn, ckvnT.re("(k p) t -> p k t", p=128))
    for h in range(8):
        inproj_fm(P, c, cqn, w_uq, 512, [(h * 192, 128)], T_ALL, qaT[h, 0:128, :])
        inproj_fm(P, c, cqn, w_uq, 512, [(h * 192 + 128, 64), (h * 192 + 160, 32), (h * 192 + 128, 32)], T_ALL, qr2T[h])
        inproj_fm(P, c, ckvn, w_ukv, 512, [(h * 256, 128)], T_ALL, knT[h])
    inproj_tok(P, c, ckvn, w_ukv, 512, 2048, T_ALL, z2)
    P.pop()
    P.push()
    c.wrr = {}
    for h in range(8):
        rope_fm(P, c, qaT[h, 128:192, :], qr2T[h, 0:64, :], qr2T[h, 64:128, :], cosT, sinST, T_ALL)
    rope_fm(P, c, krrT, krT2[0:64, :], krT2[64:128, :], cosT, sinST, T_ALL)
    P.pop()


def mla_prompt_stage(P, c, qaT, knT, krrT, z2, ident_d, maskneg_d, hmix):
    P.push()
    c.wrr = {}
    ident = P.sbuf("ident", [128, 128], F32)
    P.dma(ident, ident_d)
    maskneg = P.sbuf("maskneg", [128, 128], F32)
    P.dma(maskneg, maskneg_d)
    kr = P.sbuf("pa_kr", [64, SEQ], F32)
    P.dma(kr, krrT[:, 0:SEQ])
    NQ = SEQ // 128
    for h in range(8):
        qn = rr(c, P, "pa_qn", 2, [128, SEQ], F32)
        qr = rr(c, P, "pa_qr", 2, [64, SEQ], F32)
        kn = rr(c, P, "pa_kn", 2, [128, SEQ], F32)
        vh = rr(c, P, "pa_v", 2, [128, NQ, 128], F32)
        P.dma(qn, qaT[h, 0:128, 0:SEQ])
        P.dma(qr, qaT[h, 128:192, 0:SEQ])
        P.dma(kn, knT[h, :, 0:SEQ])
        P.dma(vh, z2[0:SEQ, h * 256 + 128:h * 256 + 256].re("(kt p) v -> p kt v", p=128))
        for qi in range(NQ):
            nk = (qi + 1) * 128
            qs = slice(qi * 128, (qi + 1) * 128)
            S = rr(c, P, "pa_S", 2, [128, SEQ], F32)
            for kb in range(0, nk, 512):
                kw = min(512, nk - kb)
                ps = next_ps(c)
                P.mm(ps[:, :kw], qn[:, qs], kn[:, kb:kb + kw], start=True, stop=False)
                P.mm(ps[:, :kw], qr[:, qs], kr[:, kb:kb + kw], start=False, stop=True)
                P.act(S[:, kb:kb + kw], ps[:, :kw], AF.Copy, scale=MLA_SCALE)
            P.tt(S[:, qi * 128:nk], S[:, qi * 128:nk], maskneg, ALU.add)
            mx = rr(c, P, "pa_mx", 2, [128, 1], F32)
            P.reduce(mx, S[:, :nk], ALU.max, axis=AX.X)
            P.ts(mx, mx, -1.0, None, ALU.mult)
            rs = rr(c, P, "pa_rs", 2, [128, 1], F32)
            P.memset(rs, 0.0)
            P.act(S[:, :nk], S[:, :nk], AF.Exp, bias=mx, accum_out=rs)
            P.recip(rs, rs)
            po = next_ps(c)
            for kt in range(qi + 1):
                if kt % 4 == 0:
                    pt = next_ps(c)
                    nblk = min(4, qi + 1 - kt)
                    for j in range(nblk):
                        P.tr(pt[:, j * 128:(j + 1) * 128], S[:, (kt + j) * 128:(kt + j + 1) * 128], ident)
                    pT = rr(c, P, "pa_pT", 3, [128, 512], F32)
                    if (kt // 4) % 2:
                        P.act(pT[:, :nblk * 128], pt[:, :nblk * 128], AF.Copy)
                    else:
                        P.copy(pT[:, :nblk * 128], pt[:, :nblk * 128])
                j = kt % 4
                P.mm(po[:, :128], pT[:, j * 128:(j + 1) * 128], vh[:, kt, :], start=(kt == 0), stop=(kt == qi))
            o = rr(c, P, "pa_o", 2, [128, 128], F32)
            P.ts(o, po[:, :128], rs[:, 0:1], None, ALU.mult)
            P.dma(hmix[qs, 1024 + h * 128:1024 + (h + 1) * 128], o, eng="pool")
    P.pop()


def mla_paged_stage(P, c, qaT, rows_out, cache, pt_d, iota_d, w_ukT_d, w_ukv, maskN_d, ident_d, hmix):
    P.push()
    c.wrr = {}
    ident32 = P.sbuf("pg_ident32", [128, 128], F32)
    P.dma(ident32, ident_d)
    ident = P.sbuf("pg_ident", [128, 128], BF16)
    P.copy(ident, ident32)
    wuk = P.sbuf("pg_wuk", [128, 8, 512], F32)
    P.dma(wuk, w_ukT_d.re("h d c -> d h c"))
    wuv = P.sbuf("pg_wuv", [128, 4, 8, 128], F32)
    wv5 = w_ukv.re("(cc p) (h two v) -> p cc h two v", p=128, h=8, two=2)
    for cc in range(4):
        P.dma(wuv[:, cc], wv5[:, cc, :, 1, :])
    NPG = PAST // 128
    pt_i = P.sbuf("pg_pti", [128, NS * NPG], I32)
    P.dma(pt_i, pt_d.re("b j -> (b j)").re("(o n) -> o n", o=1).bc([128, NS * NPG]), allow_slow_non_contiguous=True)
    ptf = P.sbuf("pg_ptf", [128, NS * NPG], F32)
    P.copy(ptf, pt_i)
    iota_p = P.sbuf("pg_iota", [128, 1], F32)
    P.dma(iota_p, iota_d)
    P.ts(ptf, ptf, 128.0, iota_p[:, 0:1], ALU.mult, ALU.add)
    idx = P.sbuf("pg_idx", [128, NS * NPG], I32)
    P.copy(idx, ptf)
    rows_s = P.sbuf("pg_rows", [64, 576], BF16)
    P.dma(rows_s, rows_out[SEQ:SEQ + TS, :], eng="pool")
    newT = P.sbuf("pg_newT", [128, 5, 64], BF16)
    for cc in range(5):
        w = 128 if cc < 4 else 64
        pt = next_psb(c)
        P.tr(pt[:w, :64], rows_s[:, cc * 128:cc * 128 + w], ident[:64, :64])
        P.copy(newT[:w, cc, :], pt[:w, :64])
    maskN = P.sbuf("pg_maskN", [32, NS, 64], F32)
    P.dma(maskN, maskN_d.re("b r k -> r b k"))

    def softmax_update(st, ps_s, n, mask):
        m, l = st
        sc = rr(c, P, "pg_sc", 2, [32, 512], F32)
        pb = rr(c, P, "pg_pb", 2, [32, 512], BF16)
        P.act(sc[:, :n], ps_s[:32, :n], AF.Copy, scale=MLA_SCALE)
        if mask is not None:
            P.tt(sc[:, :n], sc[:, :n], mask, ALU.add)
        bm = rr(c, P, "pg_bm", 2, [32, 1], F32)
        P.reduce(bm, sc[:, :n], ALU.max, axis=AX.X)
        mn = rr(c, P, "pg_mn", 2, [32, 1], F32)
        P.tt(mn, m, bm, ALU.max)
        corr = rr(c, P, "pg_corr", 2, [32, 1], F32)
        P.tt(corr, m, mn, ALU.subtract)
        P.act(corr, corr, AF.Exp)
        P.copy(m, mn)
        P.ts(mn, mn, -1.0, None, ALU.mult)
        rs = rr(c, P, "pg_rs", 2, [32, 1], F32)
        P.memset(rs, 0.0)
        P.act(pb[:, :n], sc[:, :n], AF.Exp, bias=mn, accum_out=rs)
        P.stt(l, l, corr[:, 0:1], rs, ALU.mult, ALU.add)
        return pb, corr

    for b in range(NS):
        tsl = slice(SEQ + LS * b, SEQ + LS * (b + 1))
        qsb = rr(c, P, "pg_qsb", 2, [128, 8, LS], F32)
        P.dma(qsb, qaT[:, 0:128, tsl].re("h d t -> d h t"), allow_slow_non_contiguous=True)
        qrT = rr(c, P, "pg_qrT", 2, [64, 8, LS], BF16)
        P.dma(qrT, qaT[:, 128:192, tsl].re("h r t -> r h t"), eng="pool", allow_slow_non_contiguous=True)
        qr2 = qrT.re("r h t -> r (h t)")
        ps_q = next_ps(c)
        for cc in range(4):
            for h in range(8):
                P.mm(ps_q[:, cc * 32 + h * 4:cc * 32 + h * 4 + 4], wuk[:, h, cc * 128:(cc + 1) * 128], qsb[:, h, :])
        qlT = rr(c, P, "pg_qlT", 2, [128, 4, 32], BF16)
        P.copy(qlT.re("p a r -> p (a r)"), ps_q[:, :128])
        m = rr(c, P, "pg_m", 2, [32, 1], F32)
        l = rr(c, P, "pg_l", 2, [32, 1], F32)
        acc = rr(c, P, "pg_acc", 2, [32, 512], F32)
        P.memset(m, NEG)
        P.memset(l, 0.0)
        P.memset(acc, 0.0)
        for blk in range(NPG // 4):
            pg = rr(c, P, "pg_pg", 3, [128, 4, 576], BF16)
            for j in range(4):
                k = b * NPG + blk * 4 + j
                P.generic("pool", (lambda e, o=pg[:, j, :], ix=idx[:, k:k + 1]: e.indirect_dma_start(
                    out=o.ap, out_offset=None, in_=cache.ap,
                    in_offset=bass.IndirectOffsetOnAxis(ap=ix.ap, axis=0))), [cache, idx], [pg], dma=True)
            pgT = rr(c, P, "pg_pgT", 2, [128, 5, 512], BF16)
            for cc in range(5):
                w = 128 if cc < 4 else 64
                pt = next_psb(c)
                for j in range(4):
                    P.tr(pt[:w, j * 128:(j + 1) * 128], pg[:, j, cc * 128:cc * 128 + w], ident)
                if cc % 2:
                    P.act(pgT[:w, cc, :], pt[:w, :], AF.Copy)
                else:
                    P.copy(pgT[:w, cc, :], pt[:w, :])
            ps_s = next_ps(c)
            for cc in range(4):
                P.mm(ps_s[:32, :], qlT[:, cc, :], pgT[:, cc, :], start=(cc == 0), stop=False)
            P.mm(ps_s[:32, :], qr2, pgT[:64, 4, :], start=False, stop=True)
            sc, corr = softmax_update((m, l), ps_s, 512, None)
            pt2 = next_psb(c)
            for j in range(4):
                P.tr(pt2[:, j * 32:(j + 1) * 32], sc[:, j * 128:(j + 1) * 128], ident[:32, :32])
            pT = rr(c, P, "pg_pT", 2, [128, 4, 32], BF16)
            P.copy(pT.re("p a r -> p (a r)"), pt2[:, :128])
            po = next_ps(c)
            for j in range(4):
                P.mm(po[:32, :], pT[:, j, :], pg[:, j, 0:512], start=(j == 0), stop=(j == 3))
            P.stt(acc, acc, corr[:, 0:1], po[:32, :], ALU.mult, ALU.add)
        ps_s = next_ps(c)
        for cc in range(4):
            P.mm(ps_s[:32, :64], qlT[:, cc, :], newT[:, cc, :], start=(cc == 0), stop=False)
        P.mm(ps_s[:32, :64], qr2, newT[:64, 4, :], start=False, stop=True)
        sc, corr = softmax_update((m, l), ps_s, 64, maskN[:, b, :])
        pt2 = next_psb(c)
        P.tr(pt2[:64, :32], sc[:, :64], ident[:32, :32])
        pT = rr(c, P, "pg_pT", 2, [128, 4, 32], BF16)
        P.copy(pT[:64, 0, :], pt2[:64, :32])
        po = next_ps(c)
        P.mm(po[:32, :], pT[:64, 0, :], rows_s[:, 0:512])
        P.stt(acc, acc, corr[:, 0:1], po[:32, :], ALU.mult, ALU.add)
        P.recip(l, l)
        P.ts(acc, acc, l[:, 0:1], None, ALU.mult)
        pt3 = next_ps(c)
        for cc in range(4):
            P.tr(pt3[:, cc * 32:(cc + 1) * 32], acc[:, cc * 128:(cc + 1) * 128], ident32[:32, :32])
        olT = rr(c, P, "pg_olT", 2, [128, 4, 32], F32)
        P.copy(olT.re("p a r -> p (a r)"), pt3[:, :128])
        ha = rr(c, P, "pg_ha", 2, [LS, 1024], F32)
        for half in range(2):
            pv = next_ps(c)
            for hh in range(4):
                h = half * 4 + hh
                for cc in range(4):
                    P.mm(pv[:LS, hh * 128:(hh + 1) * 128], olT[:, cc, h * 4:(h + 1) * 4], wuv[:, cc, h, :],
                         start=(cc == 0), stop=(cc == 3))
            P.copy(ha[:, half * 512:(half + 1) * 512], pv[:LS, :])
        P.dma(hmix[tsl, 1024:2048], ha)
    P.pop()


def transpose_stage(P, c, src, dst, T, F, ident_d):
    P.push()
    c.wrr = {}
    ident = P.sbuf("ts_ident", [128, 128], F32)
    P.dma(ident, ident_d)
    dv = dst.re("(k p) t -> p k t", p=128)
    k = 0
    for t0 in range(0, T, 128):
        tw = min(128, T - t0)
        xin = rr(c, P, "ts_in", 2, [128, F], F32)
        P.dma(xin[:tw], src[t0:t0 + tw, :])
        for fb in range(0, F // 128, 4):
            pt = next_ps(c)
            for j in range(4):
                P.tr(pt[:, j * 128:j * 128 + tw], xin[:tw, (fb + j) * 128:(fb + j + 1) * 128], ident[:tw, :tw])
            o = rr(c, P, "ts_o", 3, [128, 4, 128], BF16)
            k += 1
            if k % 2:
                P.copy(o[:, :, :tw], pt.re("p (a t) -> p a t", a=4)[:, :, :tw])
            else:
                P.act(o[:, :, :tw], pt.re("p (a t) -> p a t", a=4)[:, :, :tw], AF.Copy)
            P.dma(dv[:, fb:fb + 4, t0:t0 + tw], o[:, :, :tw], eng="pool")
    P.pop()


def dense_res_stage(P, c, hT_scr, wt, src, dst, F, T, blocks, scale):
    P.push()
    c.wrr = {}
    FC = F // 128
    KC = D_MODEL // 128
    TBMAX = max(b[1] for b in blocks)
    hT = P.sbuf("dr_hT", [128, FC, TBMAX], BF16)
    for (b0, bw) in blocks:
        tiles = tok_tiles(bw)
        P.dma(hT[:, :, :bw], hT_scr.re("(k p) t -> p k t", p=128)[:, :, b0:b0 + bw])
        for dc in range(KC):
            wdb = rr(c, P, "dr_w", 2, [128, FC * 128], BF16)
            P.dma(wdb, wt[dc], eng="pool")
            for (t0, tw) in tiles:
                xr = rr(c, P, "xr", 3, [128, 512], F32)
                P.dma(xr[:, :tw], src[dc * 128:(dc + 1) * 128, b0 + t0:b0 + t0 + tw])
                py = next_ps(c)
                for fc in range(FC):
                    P.mm(py[:, :tw], wdb[:, fc * 128:(fc + 1) * 128], hT[:, fc, t0:t0 + tw],
                         start=(fc == 0), stop=(fc == FC - 1))
                xo = rr(c, P, "xo", 3, [128, 512], F32)
                P.stt(xo[:, :tw], py[:, :tw], scale, xr[:, :tw], ALU.mult, ALU.add)
                P.dma(dst[dc * 128:(dc + 1) * 128, b0 + t0:b0 + t0 + tw], xo[:, :tw])
    P.pop()


def final_norm_stage(P, c, src, g_dram, dst, D, T):
    P.push()
    c.wrr = {}
    KC = D // 128
    g_sb = load_vec_fm(P, "g_fn", g_dram, D)
    for (t0, tw) in tok_tiles(T, 256):
        xo = rr(c, P, "fxo", 2, [128, KC, 256], F32)
        norm_fm(P, c, src, g_sb, xo, D, t0, tw)
        P.dma(dst.re("(k p) t -> p k t", p=128)[:, :, t0:t0 + tw], xo[:, :, :tw], eng="pool")
    P.pop()


SSM_IN = 10304
C_XBC = 4096
C_DT = 10240
CONV_DIM = 6144


def ssm_conv_stage(P, c, z3, conv_w_d, conv_b_d, conv0_d, fullP, fullS, xact, convp_out, convs_out):
    P.push()
    c.wrr = {}
    zt = P.sbuf("cv_zero", [3, CONV_DIM], F32)
    P.memset(zt, 0.0)
    P.dma(fullP[0:3, :], zt)
    P.dma(fullP[3:3 + SEQ, :], z3[0:SEQ, C_XBC:C_XBC + CONV_DIM])
    P.dma(fullS[:, 0:3, :], conv0_d)
    P.dma(fullS[:, 3:3 + LS, :], z3[SEQ:SEQ + TS, C_XBC:C_XBC + CONV_DIM].re("(b t) c -> b t c", t=LS))
    P.dma(convp_out, fullP[SEQ:SEQ + 3, :])
    P.dma(convs_out, fullS[:, LS:LS + 3, :])
    CB = 1024
    for cb in range(CONV_DIM // CB):
        cs_ = slice(cb * CB, (cb + 1) * CB)
        w_bc = rr(c, P, "cv_w", 2, [128, 4, CB], F32)
        P.dma(w_bc, conv_w_d[:, cs_].re("(o j) c -> o j c", o=1).bc([128, 4, CB]), allow_slow_non_contiguous=True)
        b_bc = rr(c, P, "cv_b", 2, [128, CB], F32)
        P.dma(b_bc, conv_b_d[cs_].re("(o c) -> o c", o=1).bc([128, CB]), allow_slow_non_contiguous=True)
        t0s = list(range(0, T_ALL, 128))
        for i0 in range(0, len(t0s), 3):
            grp = t0s[i0:i0 + 3]
            st = []
            for t0 in grp:
                tw = min(128, T_ALL - t0)
                sh = rr(c, P, "cv_sh", 3, [128, 4, CB], F32)
                if t0 < SEQ:
                    for j in range(4):
                        P.dma(sh[:, j, :], fullP[t0 + j:t0 + j + 128, cs_])
                else:
                    for j in range(4):
                        for b_ in range(NS):
                            P.dma(sh[LS * b_:LS * (b_ + 1), j, :], fullS[b_, j:j + LS, cs_])
                acc = rr(c, P, "cv_acc", 3, [128, CB], F32)
                tmp = rr(c, P, "cv_tmp", 3, [128, CB], F32)
                st.append((t0, tw, sh, acc, tmp))
            for (t0, tw, sh, acc, tmp) in st:
                P.tt(acc[:tw], sh[:tw, 0, :], w_bc[:tw, 0, :], ALU.mult)
            for (t0, tw, sh, acc, tmp) in st:
                P.tt(tmp[:tw], sh[:tw, 1, :], w_bc[:tw, 1, :], ALU.mult)
            for (t0, tw, sh, acc, tmp) in st:
                P.tt(acc[:tw], acc[:tw], b_bc[:tw], ALU.add)
            for j in range(1, 4):
                for (t0, tw, sh, acc, tmp) in st:
                    P.tt(acc[:tw], acc[:tw], tmp[:tw], ALU.add)
                if j < 3:
                    for (t0, tw, sh, acc, tmp) in st:
                        P.tt(tmp[:tw], sh[:tw, j + 1, :], w_bc[:tw, j + 1, :], ALU.mult)
            for (t0, tw, sh, acc, tmp) in st:
                P.act(acc[:tw], acc[:tw], AF.Silu)
                P.dma(xact[t0:t0 + tw, cs_], acc[:tw], eng="pool")
    P.pop()


def ssd_stage(P, c, z3, xact, dtb_d, alog_d, dsk_d, h0T_d, ident_d, maskp_d, masks_d, ss_d, sel_d, selT_d,
              yscr, ssmp_out, ssms_out):
    P.push()
    c.wrr = {}
    ident = P.sbuf("sd_ident", [128, 128], F32)
    P.dma(ident, ident_d)
    maskp = P.sbuf("sd_maskp", [64, 64], F32)
    P.dma(maskp, maskp_d)
    masks = P.sbuf("sd_masks", [64, 64], F32)
    P.dma(masks, masks_d)
    ones = P.sbuf("sd_ones", [64, 128], F32)
    P.memset(ones, 1.0)
    dtb = bcast_rows(P, "sd_dtb", dtb_d, 64, parts=64)
    A_bc = bcast_rows(P, "sd_A", alog_d, 64, parts=64)
    P.act(A_bc, A_bc, AF.Exp)
    P.ts(A_bc, A_bc, -1.0, None, ALU.mult)
    D_bc = bcast_rows(P, "sd_D", dsk_d, 64, parts=64)
    P.push()
    c.wrr = {}
    hT = [P.sbuf("sd_hT%d" % g, [128, 512], F32) for g in range(8)]
    for g in range(8):
        P.memset(hT[g], 0.0)

    def load_chunk(tsl):
        xs = rr(c, P, "sd_xs", 2, [64, 4096], F32)
        bc = rr(c, P, "sd_bc", 2, [64, 2048], F32)
        P.dma(xs, xact[tsl, 0:4096])
        P.dma(bc, xact[tsl, 4096:6144])
        dt = rr(c, P, "sd_dt", 2, [64, 64], F32)
        P.dma(dt, z3[tsl, C_DT:C_DT + 64])
        P.tt(dt, dt, dtb, ALU.add)
        P.act(dt, dt, AF.Exp)
        P.ts(dt, dt, 1.0, None, ALU.add)
        P.act(dt, dt, AF.Ln)
        dtA = rr(c, P, "sd_dtA", 2, [64, 64], F32)
        P.tt(dtA, dt, A_bc, ALU.mult)
        return xs, bc, dt, dtA

    def intra(U, xs, bc, dt, dtA, cum, g):
        R = rr(c, P, "sd_R", 2, [64, 8, 64], F32)
        P.tt(R, U.re("s (o t) -> s o t", o=1).bc([64, 8, 64]),
             dtA[:, 8 * g:8 * g + 8].re("s (h o) -> s h o", o=1).bc([64, 8, 64]), ALU.mult)
        cr = next_ps(c)
        P.mm(cr[:, :], ones, R.re("s h t -> s (h t)"))
        E = rr(c, P, "sd_E", 2, [64, 8, 64], F32)
        P.tt(E, cr[:64, :].re("s (h t) -> s h t", h=8),
             cum[:, 8 * g:8 * g + 8].re("s (h o) -> s h o", o=1).bc([64, 8, 64]), ALU.subtract)
        P.ts(E, E, 0.0, None, ALU.min)
        P.act(E, E, AF.Exp)
        P.tt(E, E, U.re("s (o t) -> s o t", o=1).bc([64, 8, 64]), ALU.mult)
        pt = next_ps(c)
        P.tr(pt[:, 0:64], bc[:, g * 128:(g + 1) * 128], ident[:64, :64])
        P.tr(pt[:, 64:128], bc[:, 1024 + g * 128:1024 + (g + 1) * 128], ident[:64, :64])
        BCT = rr(c, P, "sd_BCT", 2, [128, 128], F32)
        P.act(BCT, pt[:, 0:128], AF.Copy)
        pcb = next_ps(c)
        P.mm(pcb[:64, :64], BCT[:, 0:64], BCT[:, 64:128])
        CBs = rr(c, P, "sd_CB", 2, [64, 64], F32)
        P.act(CBs, pcb[:64, :64], AF.Copy)
        P.tt(E, E, CBs.re("s (o t) -> s o t", o=1).bc([64, 8, 64]), ALU.mult)
        P.tt(E, E, dt[:, 8 * g:8 * g + 8].re("s (h o) -> s h o", o=1).bc([64, 8, 64]), ALU.mult)
        py = next_ps(c)
        for hh in range(8):
            P.mm(py[:64, hh * 64:(hh + 1) * 64], E[:, hh, :], xs[:, (8 * g + hh) * 64:(8 * g + hh + 1) * 64])
        return py, BCT, cr

    def combine(py, pi, ecum, xs, g, ytile):
        yi = rr(c, P, "sd_yi", 2, [64, 8, 64], F32)
        P.tt(yi, pi[:64, :].re("t (h p) -> t h p", h=8),
             ecum[:, 8 * g:8 * g + 8].re("t (h o) -> t h o", o=1).bc([64, 8, 64]), ALU.mult)
        yg = ytile[:, g * 512:(g + 1) * 512].re("t (h p) -> t h p", h=8)
        P.tt(yg, py[:64, :].re("t (h p) -> t h p", h=8), yi, ALU.add)
        P.tt(yi, xs[:, g * 512:(g + 1) * 512].re("t (h p) -> t h p", h=8),
             D_bc[:, 8 * g:8 * g + 8].re("t (h o) -> t h o", o=1).bc([64, 8, 64]), ALU.mult, eng="pool")
        P.tt(yg, yg, yi, ALU.add, eng="pool")

    for ch in range(SEQ // 64):
        tsl = slice(ch * 64, (ch + 1) * 64)
        xs, bc, dt, dtA = load_chunk(tsl)
        pc = next_ps(c)
        P.mm(pc[:64, :64], maskp, dtA)
        cum = rr(c, P, "sd_cum", 2, [64, 64], F32)
        P.copy(cum, pc[:64, :64])
        ecum = rr(c, P, "sd_ecum", 2, [64, 64], F32)
        P.act(ecum, cum, AF.Exp)
        ytile = rr(c, P, "sd_y", 2, [64, 4096], F32)


        for g in range(8):
            py, BCT, cr = intra(maskp, xs, bc, dt, dtA, cum, g)
            pi = next_ps(c)
            P.mm(pi[:64, :], BCT[:, 64:128], hT[g])
            combine(py, pi, ecum, xs, g, ytile)
            cl = cr.re("p (h t) -> p h t", h=8)[:, :, 63]
            el = rr(c, P, "sd_el", 2, [128, 8], F32)
            P.act(el, cl, AF.Exp)
            dec = rr(c, P, "sd_dec", 2, [64, 8], F32)
            P.tt(dec, cl[:64], cum[:, 8 * g:8 * g + 8], ALU.subtract)
            P.act(dec, dec, AF.Exp)
            P.tt(dec, dec, dt[:, 8 * g:8 * g + 8], ALU.mult)
            xd = rr(c, P, "sd_xd", 2, [64, 8, 64], F32)
            P.tt(xd, xs[:, g * 512:(g + 1) * 512].re("t (h p) -> t h p", h=8),
                 dec.re("s (h o) -> s h o", o=1).bc([64, 8, 64]), ALU.mult, eng="pool")
            ph = next_ps(c)
            P.mm(ph[:, :], bc[:, g * 128:(g + 1) * 128], xd.re("s h p -> s (h p)"))
            h3 = hT[g].re("n (h p) -> n h p", h=8)
            P.tt(h3, h3, el.re("n (h o) -> n h o", o=1).bc([128, 8, 64]), ALU.mult)
            P.tt(hT[g], hT[g], ph[:, :], ALU.add)
        P.dma(yscr[tsl, :], ytile)
    for g in range(8):
        P.dma(ssmp_out[:, g * 512:(g + 1) * 512], hT[g])
    P.pop()
    P.push()
    c.wrr = {}
    tsl = slice(SEQ, SEQ + TS)
    SS = P.sbuf("sd_SS", [64, 64], F32)
    P.dma(SS, ss_d)
    sel = P.sbuf("sd_sel", [64, NS], F32)
    P.dma(sel, sel_d)
    selT = P.sbuf("sd_selT", [128, NS, 64], F32)
    P.dma(selT, selT_d)
    selbc = P.sbuf("sd_selbc", [64, NS, 128], F32)
    P.copy(selbc, sel.re("s (b o) -> s b o", o=1).bc([64, NS, 128]))
    xs, bc, dt, dtA = load_chunk(tsl)
    pc = next_ps(c)
    P.mm(pc[:64, :64], masks, dtA)
    cum = rr(c, P, "sd_cum", 2, [64, 64], F32)
    P.copy(cum, pc[:64, :64])
    ecum = rr(c, P, "sd_ecum", 2, [64, 64], F32)
    P.act(ecum, cum, AF.Exp)
    pc2 = next_ps(c)
    P.mm(pc2[:64, :64], SS, dtA)
    dec_all = P.sbuf("sd_decall", [64, 64], F32)
    P.tt(dec_all, pc2[:64, :64], cum, ALU.subtract)
    P.act(dec_all, dec_all, AF.Exp)
    P.tt(dec_all, dec_all, dt, ALU.mult)
    elS = P.sbuf("sd_elS", [128, NS, 64], F32)
    for b in range(NS):
        pe_ = next_ps(c)
        P.mm(pe_[:, :64], selbc[:, b, :], dtA)
        P.act(elS[:, b, :], pe_[:, :64], AF.Exp)
    ytile = rr(c, P, "sd_y", 2, [64, 4096], F32)
    for g in range(8):
        py, BCT, cr = intra(masks, xs, bc, dt, dtA, cum, g)
        h0g = rr(c, P, "sd_h0g", 1, [128, NS, 512], F32)
        P.dma(h0g, h0T_d[:, :, g * 512:(g + 1) * 512].re("b n f -> n b f"))
        CTm = rr(c, P, "sd_CTm", 2, [128, NS, 64], F32)
        P.tt(CTm, BCT[:, 64:128].re("n (o t) -> n o t", o=1).bc([128, NS, 64]), selT, ALU.mult)
        pi = next_ps(c)
        for b in range(NS):
            P.mm(pi[:64, :], CTm[:, b, :], h0g[:, b, :], start=(b == 0), stop=(b == NS - 1))
        combine(py, pi, ecum, xs, g, ytile)
        xd = rr(c, P, "sd_xd", 2, [64, 8, 64], F32)
        P.tt(xd, xs[:, g * 512:(g + 1) * 512].re("t (h p) -> t h p", h=8),
             dec_all[:, 8 * g:8 * g + 8].re("s (h o) -> s h o", o=1).bc([64, 8, 64]), ALU.mult, eng="pool")
        for b in range(NS):
            Bm = rr(c, P, "sd_Bm", 3, [64, 128], F32)
            P.ts(Bm, bc[:, g * 128:(g + 1) * 128], sel[:, b:b + 1], None, ALU.mult, eng="pool")
            ph = next_ps(c)
            P.mm(ph[:, :], Bm, xd.re("s h p -> s (h p)"))
            hn = rr(c, P, "sd_hn", 3, [128, 8, 64], F32)
            P.tt(hn, h0g[:, b, :].re("n (h p) -> n h p", h=8),
                 elS[:, b, 8 * g:8 * g + 8].re("n (h o) -> n h o", o=1).bc([128, 8, 64]), ALU.mult)
            P.tt(hn, hn, ph[:, :].re("n (h p) -> n h p", h=8), ALU.add)
            P.dma(ssms_out[b, :, g * 512:(g + 1) * 512], hn.re("n h p -> n (h p)"))
    P.dma(yscr[tsl, :], ytile)
    P.pop()
    P.pop()


def ssm_gate_stage(P, c, yscr, z3, sg_d, ygn):
    P.push()
    c.wrr = {}
    g_bc = bcast_rows(P, "sg_bc", sg_d, 4096)
    for t0 in range(0, T_ALL, 128):
        tw = min(128, T_ALL - t0)
        y = rr(c, P, "sg_y", 2, [128, 4096], F32)
        zz = rr(c, P, "sg_z", 2, [128, 4096], F32)
        P.dma(y[:tw], yscr[t0:t0 + tw, :])
        P.dma(zz[:tw], z3[t0:t0 + tw, 0:4096])
        P.act(zz[:tw], zz[:tw], AF.Silu)
        P.tt(y[:tw], y[:tw], zz[:tw], ALU.mult)
        ss = rr(c, P, "sg_ss", 2, [128, 8], F32)
        P.memset(ss[:tw], 0.0)
        for g in range(8):
            P.act(zz[:tw, g * 512:(g + 1) * 512], y[:tw, g * 512:(g + 1) * 512], AF.Square, accum_out=ss[:tw, g:g + 1])
        P.ts(ss[:tw], ss[:tw], 1.0 / 512, EPS, ALU.mult, ALU.add)
        P.act(ss[:tw], ss[:tw], AF.Sqrt)
        P.recip(ss[:tw], ss[:tw])
        for g in range(8):
            P.stt(y[:tw, g * 512:(g + 1) * 512], y[:tw, g * 512:(g + 1) * 512], ss[:tw, g:g + 1],
                  g_bc[:tw, g * 512:(g + 1) * 512], ALU.mult, ALU.mult)
        P.dma(ygn[t0:t0 + tw, :], y[:tw], eng="pool")
    P.pop()


def gates_fm2(P, c, xn, W, T, igT, fgT):
    KC = D_MODEL // 128
    wg8 = P.sbuf("wgate8", [128, KC, 8], BF16)
    P.dma(wg8, W.re("(k p) n -> p k n", p=128)[:, :, C_IG:C_IG + 8], eng="pool")
    for (t0, tw) in tok_tiles(T):
        for j, dst in ((0, igT), (1, fgT)):
            ps = next_ps(c)
            for kc in range(KC):
                P.mm(ps[:4, :tw], wg8[:, kc, 4 * j:4 * j + 4], xn[:, kc, t0:t0 + tw],
                     start=(kc == 0), stop=(kc == KC - 1))
            P.copy(dst[:, t0:t0 + tw], ps[:4, :tw])


SKIP_PAGED = False


def build_program():
    nc = bass.Bass("TRN2", target_bir_lowering=False)
    P = Prog(nc)
    c = Ctx()
    I = lambda n, s, d=F32: P.dram(n, s, d, kind="ExternalInput")
    O = lambda n, s, d=F32: P.dram(n, s, d, kind="ExternalOutput")
    xT_in = I("xT_in", [D_MODEL, T_ALL])
    norm_g = I("norm_g", [6, D_MODEL])
    wg = [I("wg%d" % i, [D_FF // 128, 128, D_MODEL]) for i in range(4)]
    wu = [I("wu%d" % i, [D_FF // 128, 128, D_MODEL]) for i in range(4)]
    wd = [I("wd%d" % i, [D_MODEL // 128, 128, D_FF]) for i in range(4)]
    w_in = I("w_in_even", [D_MODEL, EVEN_IN])
    kvg = I("kv_g", [512])
    cs = I("cs_tab", [T_ALL, 64])
    big = I("b_ig", [4])
    bfg = I("b_fg", [4])
    ct0 = I("ct0", [NS, 4, 128, 257])
    m0s = I("m0s", [4, NS])
    sel = I("sel", [TS, NS])
    mg = I("mlstm_g", [1024])
    maskp = I("maskp", [64, 64])
    masks = I("masks", [64, 64])
    selT = I("selT", [128, NS, 64])
    qg = I("mla_q_g", [512])
    w_uq = I("w_uq", [512, 1536])
    w_ukv = I("w_ukv", [512, 2048])
    cosT = I("cosT", [64, T_ALL])
    sinST = I("sinST", [64, T_ALL])
    ident_d = I("ident", [128, 128])
    maskneg_d = I("maskneg", [128, 128])
    if SKIP_PAGED:
        dbg_ha = I("dbg_ha", [TS, 1024])
    else:
        cache = I("cache", [10240 * 128, 576])
    pt_d = I("page_tab", [NS, PAST // 128], I32)
    iota_d = I("iota_p", [128, 1])
    w_ukT_d = I("w_ukT", [8, 128, 512])
    maskN_d = I("maskN", [NS, 32, 64])
    w_oe = I("w_out_even_t", [D_MODEL // 128, 128, 2048])
    w_in_ssm = I("w_in_ssm", [D_MODEL, SSM_IN])
    conv_w = I("conv_w", [4, CONV_DIM])
    conv_b = I("conv_b", [CONV_DIM])
    conv0 = I("conv0", [NS, 3, CONV_DIM])
    dtb = I("dt_bias", [64])
    alog = I("A_log", [64])
    dsk = I("D_skip", [64])
    h0T = I("h0T", [NS, 128, 4096])
    ss_d = I("ss_mat", [64, 64])
    sg = I("ssm_g", [4096])
    w_os = I("w_out_ssm_t", [D_MODEL // 128, 128, 4096])
    fg = I("final_g", [D_MODEL])
    yT_out = O("yT", [D_MODEL, T_ALL])
    rows_out = O("rows", [T_ALL, 576])
    ctp_out = O("ctp", [4, 128, 257])
    cts_out = O("cts", [NS, 4, 128, 257])
    mp_out = O("mp", [4, 1])
    ms_out = O("ms", [4, NS])
    ssmp_out = O("ssmp", [128, 4096])
    ssms_out = O("ssms", [NS, 128, 4096])
    convp_out = O("convp", [3, CONV_DIM])
    convs_out = O("convs", [NS, 3, CONV_DIM])
    xa = P.dram("xT_a", [D_MODEL, T_ALL], F32)
    xb = P.dram("xT_b", [D_MODEL, T_ALL], F32)
    xn_scr = P.dram("xn_scr", [D_MODEL, T_ALL], BF16)
    z = P.dram("z_even", [T_ALL, EVEN_IN], F32)
    qkT = P.dram("qkT", [1024, T_ALL], F32)
    cqT = P.dram("cqT", [512, T_ALL], F32)
    ckvT = P.dram("ckvT", [512, T_ALL], F32)
    krT2 = P.dram("krT2", [128, T_ALL], F32)
    hmix = P.dram("hmix", [T_ALL, 2048], F32, kind=("ExternalOutput" if DEBUG else "Internal"))
    cqnT = P.dram("cqnT", [512, T_ALL], BF16)
    ckvnT = P.dram("ckvnT", [512, T_ALL], BF16)
    qaT = P.dram("qaT", [8, 192, T_ALL], F32)
    qr2T = P.dram("qr2T", [8, 128, T_ALL], F32)
    knT = P.dram("knT", [8, 128, T_ALL], F32)
    krrT = P.dram("krrT", [64, T_ALL], F32)
    z2 = P.dram("z2", [T_ALL, 2048], F32)
    hmixT = P.dram("hmixT", [2048, T_ALL], BF16)
    z3 = P.dram("z3", [T_ALL, SSM_IN], F32)
    fullP = P.dram("fullP", [SEQ + 3, CONV_DIM], F32)
    fullS = P.dram("fullS", [NS, LS + 3, CONV_DIM], F32)
    xact = P.dram("xact", [T_ALL, CONV_DIM], F32)
    yscr = P.dram("yscr", [T_ALL, 4096], F32)
    ygn = P.dram("ygn", [T_ALL, 4096], F32)
    ygnT = P.dram("ygnT", [4096, T_ALL], BF16)
    setup_common(P, c)
    ffn_stage(P, c, xT_in, xa, xn_scr, norm_g[0], wg[0], wu[0], wd[0], D_MODEL, D_FF, T_ALL, BLOCKS)
    norm_stage(P, c, xa, norm_g[1], xn_scr, D_MODEL, T_ALL)
    even_front_stage(P, c, xn_scr, w_in, kvg, cs, big, bfg, mg, ct0, m0s, sel, maskp, masks, selT, z, qkT, cqT, ckvT,
                     krT2, hmix, rows_out, ctp_out, cts_out, mp_out, ms_out)
    mla_prep_stage(P, c, cqT, ckvT, krT2, qg, kvg, w_uq, w_ukv, cosT, sinST, cqnT, ckvnT, qaT, qr2T, knT, krrT, z2)
    mla_prompt_stage(P, c, qaT, knT, krrT, z2, ident_d, maskneg_d, hmix)
    if SKIP_PAGED:
        P.dma(hmix[SEQ:SEQ + TS, 1024:2048], dbg_ha)
    else:
        mla_paged_stage(P, c, qaT, rows_out, cache, pt_d, iota_d, w_ukT_d, w_ukv, maskN_d, ident_d, hmix)
    transpose_stage(P, c, hmix, hmixT, T_ALL, 2048, ident_d)
    dense_res_stage(P, c, hmixT, w_oe, xa, xb, 2048, T_ALL, BLOCKS, 1.0)
    ffn_stage(P, c, xb, xa, xn_scr, norm_g[2], wg[1], wu[1], wd[1], D_MODEL, D_FF, T_ALL, BLOCKS)
    ffn_stage(P, c, xa, xb, xn_scr, norm_g[3], wg[2], wu[2], wd[2], D_MODEL, D_FF, T_ALL, BLOCKS)
    norm_stage(P, c, xb, norm_g[4], xn_scr, D_MODEL, T_ALL)
    P.push()
    c.wrr = {}
    xnf = P.sbuf("xnf2", [128, D_MODEL // 128, T_ALL], BF16)
    P.dma(xnf, xn_scr.re("(k p) t -> p k t", p=128))
    inproj_tok(P, c, xnf, w_in_ssm, D_MODEL, SSM_IN, T_ALL, z3)
    P.pop()
    ssm_conv_stage(P, c, z3, conv_w, conv_b, conv0, fullP, fullS, xact, convp_out, convs_out)
    ssd_stage(P, c, z3, xact, dtb, alog, dsk, h0T, ident_d, maskp, masks, ss_d, sel, selT, yscr, ssmp_out, ssms_out)
    ssm_gate_stage(P, c, yscr, z3, sg, ygn)
    transpose_stage(P, c, ygn, ygnT, T_ALL, 4096, ident_d)
    dense_res_stage(P, c, ygnT, w_os, xb, xa, 4096, T_ALL, BLOCKS, 1.0)
    ffn_stage(P, c, xa, xb, xn_scr, norm_g[5], wg[3], wu[3], wd[3], D_MODEL, D_FF, T_ALL, BLOCKS)
    final_norm_stage(P, c, xb, fg, yT_out, D_MODEL, T_ALL)
    P.finish(None)
    return nc, P


def rope_table():
    half = 32
    inv = (np.float32(10000.0) ** (-np.arange(half, dtype=np.float32) / np.float32(half))).astype(np.float32)
    pos = np.concatenate([np.arange(SEQ), np.tile(PAST + np.arange(LS), NS)]).astype(np.float32)
    ang = (pos[:, None] * inv[None, :]).astype(np.float32)
    return np.concatenate([np.cos(ang), np.sin(ang)], axis=1).astype(np.float32)


def kernel(x_prompt, x_sample, cache_mla, state_mlstm_C, state_mlstm_n, state_mlstm_m, state_ssm,
           state_conv, page_table, norm_g, ffn_w_gate, ffn_w_up, ffn_w_down, w_in_even, b_ig, b_fg,
           mlstm_norm_g, mla_q_norm_g, mla_kv_norm_g, w_uq, w_ukv, w_out_even, w_in_ssm, conv_w,
           conv_b, dt_bias, A_log, D_skip, ssm_norm_g, w_out_ssm, final_norm_g):
    f32 = np.float32
    A = lambda a: np.ascontiguousarray(np.asarray(a), dtype=None)
    x_prompt = np.asarray(x_prompt); x_sample = np.asarray(x_sample)
    nc, P = build_program()
    shared = {
        "norm_g": A(np.asarray(norm_g).reshape(6, D_MODEL)),
        "w_out_even_t": tile_w(np.asarray(w_out_even)),
        "w_in_ssm": A(w_in_ssm), "conv_w": A(conv_w), "conv_b": A(conv_b), "dt_bias": A(dt_bias),
        "A_log": A(A_log), "D_skip": A(D_skip), "ssm_g": A(ssm_norm_g), "final_g": A(final_norm_g),
        "w_out_ssm_t": tile_w(np.asarray(w_out_ssm)),
        "ss_mat": np.kron(np.eye(NS, dtype=f32), np.ones((LS, LS), f32)),
        "w_in_even": A(w_in_even),
        "kv_g": A(mla_kv_norm_g),
        "cs_tab": rope_table(),
        "b_ig": A(b_ig), "b_fg": A(b_fg),
        "sel": np.repeat(np.eye(NS, dtype=f32), LS, axis=0),
        "mlstm_g": A(mlstm_norm_g),
        "mla_q_g": A(mla_q_norm_g), "w_uq": A(w_uq), "w_ukv": A(w_ukv),
        "ident": np.eye(128, dtype=f32),
        "cache": (None if SKIP_PAGED else np.asarray(cache_mla).reshape(10240 * 128, 576)),
        "iota_p": np.arange(128, dtype=f32).reshape(128, 1),
        "w_ukT": np.ascontiguousarray(np.asarray(w_ukv).reshape(512, 8, 256)[:, :, :128].transpose(1, 2, 0)),
        "maskN": np.ascontiguousarray(np.stack([np.where((np.arange(64)[None, :] // LS == b) & ((np.arange(64)[None, :] % LS) <= (np.arange(32)[:, None] % LS)), 0.0, NEG) for b in range(NS)]).astype(f32)),
        "maskneg": np.where(np.arange(128)[None, :] <= np.arange(128)[:, None], 0.0, NEG).astype(f32),
        "maskp": np.triu(np.ones((64, 64), f32)),
        "masks": (np.triu(np.ones((64, 64), f32)) * np.kron(np.eye(NS, dtype=f32), np.ones((LS, LS), f32))),
        "selT": np.ascontiguousarray(np.broadcast_to(np.repeat(np.eye(NS, dtype=f32), LS, axis=1)[None], (128, NS, 64))),
    }
    for i, (l, k) in enumerate([(0, 0), (0, 1), (1, 0), (1, 1)]):
        shared["wg%d" % i] = tile_w(np.asarray(ffn_w_gate)[l, k])
        shared["wu%d" % i] = tile_w(np.asarray(ffn_w_up)[l, k])
        shared["wd%d" % i] = tile_w(np.asarray(ffn_w_down)[l, k])
    if SKIP_PAGED:
        del shared["cache"]
    cs = shared["cs_tab"]
    shared["cosT"] = np.ascontiguousarray(np.concatenate([cs[:, :32], cs[:, :32]], axis=1).T)
    shared["sinST"] = np.ascontiguousarray(np.concatenate([-cs[:, 32:], cs[:, 32:]], axis=1).T)
    sC = np.asarray(state_mlstm_C); sn = np.asarray(state_mlstm_n); sm = np.asarray(state_mlstm_m)
    in_maps = []
    for c in range(8):
        xp = x_prompt[c] if c < 4 else np.zeros((SEQ, D_MODEL), f32)
        xs = x_sample[NS * c:NS * (c + 1)].reshape(TS, D_MODEL)
        xT = np.ascontiguousarray(np.concatenate([xp, xs], axis=0).T)
        ct0 = np.concatenate([sC[NS * c:NS * (c + 1)].transpose(0, 1, 3, 2),
                              sn[NS * c:NS * (c + 1)][..., None]], axis=-1)
        m = dict(shared)
        m["xT_in"] = xT
        m["ct0"] = np.ascontiguousarray(ct0, dtype=f32)
        m["m0s"] = np.ascontiguousarray(sm[NS * c:NS * (c + 1)].T, dtype=f32)
        m["page_tab"] = np.ascontiguousarray(np.asarray(page_table)[NS * c:NS * (c + 1)], dtype=np.int32)
        m["conv0"] = np.ascontiguousarray(np.asarray(state_conv)[NS * c:NS * (c + 1)], dtype=f32)
        m["h0T"] = np.ascontiguousarray(np.asarray(state_ssm)[NS * c:NS * (c + 1)].transpose(0, 3, 1, 2).reshape(NS, 128, 4096))
        if SKIP_PAGED:
            m["dbg_ha"] = DBG_HA if c == 0 else np.zeros((TS, 1024), f32)
        in_maps.append(m)
    res = run_bass_kernel_spmd(nc, in_maps, core_ids=list(range(8)))
    R = res.results
    kernel.last = R
    B, DB = 4, 128
    y_p = np.zeros((B, SEQ, D_MODEL), f32); y_s = np.zeros((DB, LS, D_MODEL), f32)
    rows_p = np.zeros((B, SEQ, 576), f32); rows_s = np.zeros((DB, LS, 576), f32)
    C_p = np.zeros((B, 4, 256, 128), f32); C_s = np.zeros((DB, 4, 256, 128), f32)
    n_p = np.zeros((B, 4, 128), f32); n_s = np.zeros((DB, 4, 128), f32)
    m_p = np.zeros((B, 4), f32); m_s = np.zeros((DB, 4), f32)
    ssm_p = np.zeros((B, 64, 64, 128), f32); ssm_s = np.zeros((DB, 64, 64, 128), f32)
    conv_p = np.zeros((B, 3, 6144), f32); conv_s = np.zeros((DB, 3, 6144), f32)
    for c in range(8):
        r = R[c]
        sl = slice(NS * c, NS * (c + 1))
        rows_s[sl] = r["rows"][SEQ:].reshape(NS, LS, 576)
        y_s[sl] = r["yT"][:, SEQ:].T.reshape(NS, LS, D_MODEL)
        ssm_s[sl] = r["ssms"].reshape(NS, 128, 64, 64).transpose(0, 2, 3, 1)
        conv_s[sl] = r["convs"]
        C_s[sl] = r["cts"][..., :256].transpose(0, 1, 3, 2)
        n_s[sl] = r["cts"][..., 256]
        m_s[sl] = r["ms"].T
        if c < 4:
            rows_p[c] = r["rows"][:SEQ]
            y_p[c] = r["yT"][:, :SEQ].T
            ssm_p[c] = r["ssmp"].reshape(128, 64, 64).transpose(1, 2, 0)
            conv_p[c] = r["convp"]
            C_p[c] = r["ctp"][..., :256].transpose(0, 2, 1)
            n_p[c] = r["ctp"][..., 256]
            m_p[c] = r["mp"][:, 0]
    return (y_p, y_s, rows_p, rows_s, C_p, C_s, n_p, n_s, m_p, m_s, ssm_p, ssm_s, conv_p, conv_s)
```
